# Optimizing a Trainium2 kernel written in Bass

```python
import math
import jax
import jax.numpy as jnp
from jax import lax
import numpy as np

D_MODEL = 2048
BATCH = 2
SEQ = 8192
DEPTH = 4
DEC_BATCH = 4
DEC_SEQ = 2048
PAST_LEN = 128

HEAD_DIM = 64
GRID_W = 64
NA_HEADS = 8
NA_ROWS = 8
NA_COLS = 16
DIL_PATTERNS = ((128, 1), (512, 4), (2048, 16))
DIL_HEADS_PER_GROUP = 4
DIL_HEADS = 12
DIL_BLOCK = 64
GA_Q_HEADS = 8
GA_KV_HEADS = 2
GA_BLOCK = 128
ROPE_THETA = 10000.0
SW_Q_HEADS = 8
SW_KV_HEADS = 2
SW_HALF_WINDOW = 128
SW_BLOCK = 128

N_BRANCHES = 4
D_FF = 4 * D_MODEL
RMS_EPS = 1e-6
NEG_INF = -1e30

A_W = NA_HEADS * HEAD_DIM
B_W = DIL_HEADS * HEAD_DIM
B_OUT = DIL_HEADS_PER_GROUP * HEAD_DIM
C_QW = GA_Q_HEADS * HEAD_DIM
C_KVW = GA_KV_HEADS * HEAD_DIM
D_QW = SW_Q_HEADS * HEAD_DIM
D_KVW = SW_KV_HEADS * HEAD_DIM
GATE_W = N_BRANCHES * D_MODEL
SPLIT_SIZES = (A_W, A_W, A_W, B_W, B_W, B_W, C_QW, C_KVW, C_KVW, D_QW, D_KVW, D_KVW, GATE_W)
IN_WIDTH = 3 * A_W + 3 * B_W + C_QW + 2 * C_KVW + D_QW + 2 * D_KVW + GATE_W

kernel_name = "hybrid_natten_dilated_axial_sink_encoder"


def rmsnorm(x, g):
    xf = x.astype(jnp.float32)
    y = xf * lax.rsqrt(jnp.mean(xf * xf, axis=-1, keepdims=True) + RMS_EPS)
    return (y * g.astype(jnp.float32)).astype(x.dtype)


def alibi_slopes(n):
    return (2.0 ** (-8.0 * np.arange(1, n + 1) / n)).astype(np.float32)


def axial_rope(x):
    L = x.shape[1]
    t = jnp.arange(L)
    half = HEAD_DIM // 2

    def rot(xp, pos):
        npair = xp.shape[-1] // 2
        freqs = ROPE_THETA ** (-jnp.arange(npair, dtype=jnp.float32) / npair)
        ang = pos.astype(jnp.float32)[:, None] * freqs[None, :]
        cos = jnp.cos(ang)[None, :, None, :].astype(x.dtype)
        sin = jnp.sin(ang)[None, :, None, :].astype(x.dtype)
        x1, x2 = xp[..., :npair], xp[..., npair:]
        return jnp.concatenate([x1 * cos - x2 * sin, x2 * cos + x1 * sin], axis=-1)

    return jnp.concatenate([rot(x[..., :half], t // GRID_W), rot(x[..., half:], t % GRID_W)], axis=-1)


def neighborhood_attention(q, k, v, rel_bias):
    B, L, H, Dh = q.shape
    R = L // GRID_W
    KH = min(NA_ROWS, R)
    NCB = GRID_W // NA_COLS
    KWB = 2 * NA_COLS
    qc = np.arange(GRID_W).reshape(NCB, NA_COLS)
    kc = np.clip(np.arange(NCB) * NA_COLS - NA_COLS // 2, 0, GRID_W - KWB)[:, None] + np.arange(KWB)
    c0 = np.clip(qc - NA_COLS // 2, 0, GRID_W - NA_COLS)
    col_valid = (kc[:, None, :] >= c0[..., None]) & (kc[:, None, :] < c0[..., None] + NA_COLS)
    dc = np.clip(kc[:, None, :] - qc[..., None], -(NA_COLS - 1), NA_COLS - 1) + NA_COLS - 1
    col_bias = rel_bias[:, :, dc]
    kg = k.reshape(B, R, GRID_W, H, Dh)
    vg = v.reshape(B, R, GRID_W, H, Dh)
    qg = q.reshape(B, R, NCB, NA_COLS, H, Dh)
    scale = Dh ** -0.5

    def row_step(r):
        r0 = jnp.clip(r - KH // 2, 0, R - KH)
        k_blk = lax.dynamic_slice_in_dim(kg, r0, KH, axis=1)[:, :, kc]
        v_blk = lax.dynamic_slice_in_dim(vg, r0, KH, axis=1)[:, :, kc]
        q_row = lax.dynamic_index_in_dim(qg, r, axis=1, keepdims=False)
        s = jnp.einsum('bnqhd,bknjhd->bhnqkj', q_row, k_blk).astype(jnp.float32) * scale
        dr = r0 + jnp.arange(KH) - r + NA_ROWS - 1
        bias = jnp.transpose(col_bias[:, dr], (0, 2, 3, 1, 4)).astype(jnp.float32)
        s = jnp.where(col_valid[None, None, :, :, None, :], s + bias[None], NEG_INF)
        p = jax.nn.softmax(s.reshape(B, H, NCB, NA_COLS, KH * KWB), axis=-1).reshape(s.shape)
        o = jnp.einsum('bhnqkj,bknjhd->bnqhd', p.astype(v.dtype), v_blk)
        return o.reshape(B, GRID_W, H, Dh)

    out = lax.map(row_step, jnp.arange(R))
    return jnp.moveaxis(out, 0, 1).reshape(B, L, H * Dh)


def banded_attention(q, k, v, half_window, block, slopes, dist_scale, sink):
    N, n, Hq, Dh = q.shape
    Hkv = k.shape[2]
    G = Hq // Hkv
    nb = -(-n // block)
    npad = nb * block
    qp = jnp.pad(q, ((0, 0), (0, npad - n), (0, 0), (0, 0))).reshape(N, nb, block, Hkv, G, Dh)

    def windows(t):
        tp = jnp.pad(t, ((0, 0), (block, npad - n + block), (0, 0), (0, 0))).reshape(N, nb + 2, block, Hkv, Dh)
        return jnp.concatenate([tp[:, :-2], tp[:, 1:-1], tp[:, 2:]], axis=2)

    kw = windows(k)
    vw = windows(v)
    s = jnp.einsum('nbqkgd,nbskd->nbkgqs', qp, kw).astype(jnp.float32) * (Dh ** -0.5)
    delta = np.arange(3 * block)[None, :] - block - np.arange(block)[:, None]
    kpos = (np.arange(nb)[:, None] - 1) * block + np.arange(3 * block)[None, :]
    valid = (np.abs(delta)[None] <= half_window) & (kpos[:, None, :] >= 0) & (kpos[:, None, :] < n)
    bias = -slopes.reshape(Hkv, G)[:, :, None, None] * (dist_scale * np.abs(delta)).astype(np.float32)[None, None]
    s = jnp.where(valid[None, :, None, None], s + bias, NEG_INF)
    m = jnp.max(s, axis=-1)
    if sink is not None:
        sk = sink.astype(jnp.float32).reshape(Hkv, G)[:, :, None]
        m = jnp.maximum(m, sk)
    p = jnp.exp(s - m[..., None])
    denom = jnp.sum(p, axis=-1)
    if sink is not None:
        denom = denom + jnp.exp(sk - m)
    o = jnp.einsum('nbkgqs,nbskd->nbqkgd', (p / denom[..., None]).astype(v.dtype), vw)
    lse = jnp.transpose(m + jnp.log(denom), (0, 1, 4, 2, 3))
    return (o.reshape(N, npad, Hq, Dh)[:, :n], lse.reshape(N, npad, Hq)[:, :n])


def dilated_attention(q, k, v):
    B, L, _, Dh = q.shape
    slopes = alibi_slopes(DIL_HEADS)
    hg = DIL_HEADS_PER_GROUP
    outs = []
    lses = []
    for g, (w, d) in enumerate(DIL_PATTERNS):
        n = L // d
        sl = slice(g * hg, (g + 1) * hg)

        def fold(t):
            return t.reshape(B, n, d, hg, Dh).swapaxes(1, 2).reshape(B * d, n, hg, Dh)

        o, lse = banded_attention(fold(q[:, :, sl]), fold(k[:, :, sl]), fold(v[:, :, sl]),
                                  w // (2 * d), DIL_BLOCK, slopes[sl], d, None)
        outs.append(o.reshape(B, d, n, hg, Dh).swapaxes(1, 2).reshape(B, L, hg, Dh))
        lses.append(lse.reshape(B, d, n, hg).swapaxes(1, 2).reshape(B, L, hg))
    wts = jax.nn.softmax(jnp.stack(lses, axis=0), axis=0)
    out = jnp.sum(wts[..., None] * jnp.stack(outs, axis=0).astype(jnp.float32), axis=0)
    return out.astype(q.dtype).reshape(B, L, hg * Dh)


def axial_global_attention(q, k, v, q_gain, k_gain):
    B, L, Hq, Dh = q.shape
    Hkv = k.shape[2]
    G = Hq // Hkv
    q = axial_rope(rmsnorm(q, q_gain))
    k = axial_rope(rmsnorm(k, k_gain))
    nqb = L // GA_BLOCK
    qb = jnp.moveaxis(q.reshape(B, nqb, GA_BLOCK, Hkv, G, Dh), 1, 0)
    scale = Dh ** -0.5

    def step(qblk):
        s = jnp.einsum('bqkgd,bskd->bkgqs', qblk, k).astype(jnp.float32) * scale
        p = jax.nn.softmax(s, axis=-1)
        return jnp.einsum('bkgqs,bskd->bqkgd', p.astype(v.dtype), v)

    o = lax.map(step, qb)
    return jnp.moveaxis(o, 0, 1).reshape(B, L, Hq * Dh)


def mixer_sublayer(x, norm_g, w_in, rel_bias_a, q_gain_c, k_gain_c, sink_d, w_a, w_b, w_c, w_d, w_out):
    B, L, _ = x.shape
    h = rmsnorm(x, norm_g)
    parts = jnp.split(h @ w_in, np.cumsum(SPLIT_SIZES)[:-1].tolist(), axis=-1)
    aq, ak, av, bq, bk, bv, cq, ck, cv, dq, dk, dv, gl = parts

    def hd(t):
        return t.reshape(B, L, -1, HEAD_DIM)

    ya = neighborhood_attention(hd(aq), hd(ak), hd(av), rel_bias_a)
    yb = dilated_attention(hd(bq), hd(bk), hd(bv))
    yc = axial_global_attention(hd(cq), hd(ck), hd(cv), q_gain_c, k_gain_c)
    yd, _ = banded_attention(hd(dq), hd(dk), hd(dv), SW_HALF_WINDOW, SW_BLOCK,
                             alibi_slopes(SW_Q_HEADS), 1, sink_d)
    yd = yd.reshape(B, L, D_QW)
    g = jax.nn.sigmoid(gl).reshape(B, L, N_BRANCHES, D_MODEL)
    merged = (g[:, :, 0] * (ya @ w_a) + g[:, :, 1] * (yb @ w_b)
              + g[:, :, 2] * (yc @ w_c) + g[:, :, 3] * (yd @ w_d))
    return x + merged @ w_out


def mlp_sublayer(x, norm_g, w_up, w_down):
    h = rmsnorm(x, norm_g)
    return x + jnp.square(jax.nn.relu(h @ w_up)) @ w_down


def trunk(x, norm_mix, w_in, rel_bias_a, q_gain_c, k_gain_c, sink_d, w_branch_a, w_branch_b,
          w_branch_c, w_branch_d, w_out, norm_mlp, w_up, w_down, norm_final):
    for l in range(DEPTH):
        x = mixer_sublayer(x, norm_mix[l], w_in[l], rel_bias_a[l], q_gain_c[l], k_gain_c[l], sink_d[l],
                           w_branch_a[l], w_branch_b[l], w_branch_c[l], w_branch_d[l], w_out[l])
        x = mlp_sublayer(x, norm_mlp[l], w_up[l], w_down[l])
    return rmsnorm(x, norm_final)


def setup_inputs(seed: int = 0) -> dict:
    key = jax.random.key(seed)
    ks = jax.random.split(key, 18)

    def nrm(k, shape, scale):
        return jax.random.normal(k, shape, jnp.float32) * scale

    return {
        "x_prompt": nrm(ks[0], (BATCH, SEQ, D_MODEL), 1.0),
        "x_sample": nrm(ks[1], (DEC_BATCH, DEC_SEQ, D_MODEL), 1.0),
        "norm_mix": 1.0 + nrm(ks[2], (DEPTH, D_MODEL), 0.02),
        "w_in": nrm(ks[3], (DEPTH, D_MODEL, IN_WIDTH), D_MODEL ** -0.5),
        "rel_bias_a": nrm(ks[4], (DEPTH, NA_HEADS, 2 * NA_ROWS - 1, 2 * NA_COLS - 1), 0.1),
        "q_gain_c": 1.0 + nrm(ks[5], (DEPTH, HEAD_DIM), 0.02),
        "k_gain_c": 1.0 + nrm(ks[6], (DEPTH, HEAD_DIM), 0.02),
        "sink_d": nrm(ks[7], (DEPTH, SW_Q_HEADS), 0.5),
        "w_branch_a": nrm(ks[8], (DEPTH, A_W, D_MODEL), A_W ** -0.5),
        "w_branch_b": nrm(ks[9], (DEPTH, B_OUT, D_MODEL), B_OUT ** -0.5),
        "w_branch_c": nrm(ks[10], (DEPTH, C_QW, D_MODEL), C_QW ** -0.5),
        "w_branch_d": nrm(ks[11], (DEPTH, D_QW, D_MODEL), D_QW ** -0.5),
        "w_out": nrm(ks[12], (DEPTH, D_MODEL, D_MODEL), D_MODEL ** -0.5),
        "norm_mlp": 1.0 + nrm(ks[13], (DEPTH, D_MODEL), 0.02),
        "w_up": nrm(ks[14], (DEPTH, D_MODEL, D_FF), D_MODEL ** -0.5),
        "w_down": nrm(ks[15], (DEPTH, D_FF, D_MODEL), D_FF ** -0.5),
        "norm_final": 1.0 + nrm(ks[16], (D_MODEL,), 0.02),
    }


def reference(x_prompt, x_sample, norm_mix, w_in, rel_bias_a, q_gain_c, k_gain_c, sink_d, w_branch_a,
              w_branch_b, w_branch_c, w_branch_d, w_out, norm_mlp, w_up, w_down, norm_final):
    y_prompt = trunk(x_prompt, norm_mix, w_in, rel_bias_a, q_gain_c, k_gain_c, sink_d, w_branch_a, w_branch_b,
                     w_branch_c, w_branch_d, w_out, norm_mlp, w_up, w_down, norm_final)
    y_sample = trunk(x_sample, norm_mix, w_in, rel_bias_a, q_gain_c, k_gain_c, sink_d, w_branch_a, w_branch_b,
                     w_branch_c, w_branch_d, w_out, norm_mlp, w_up, w_down, norm_final)
    return (y_prompt, y_sample)
```

```python
import contextlib
import os
import numpy as np
import ml_dtypes
import concourse.bass as bass
import concourse.mybir as mybir
from concourse.bass_utils import run_bass_kernel_spmd

F32 = mybir.dt.float32
BF16 = mybir.dt.bfloat16
AF = mybir.ActivationFunctionType
ALU = mybir.AluOpType

D = 2048
KC = 16
DFF = 8192
AQ, AK, AV, BQ, BK, BV, CQ, CK, CV, DQ, DK, DV, GL = 0, 512, 1024, 1536, 2304, 3072, 3840, 4352, 4480, 4608, 5120, 5248, 5376
INW = 13568
NEG = -30000.0
EPS = 1e-6
FMCOLS = ([AQ + 128 * i for i in range(4)] + [AK + 128 * i for i in range(4)] + [BQ + 128 * i for i in range(6)]
          + [BK + 128 * i for i in range(6)] + [CQ + 128 * i for i in range(4)] + [CK]
          + [DQ + 128 * i for i in range(4)] + [DK])
QA0, KA0, QB0, KB0, QC0, KC0, QD0, KD0 = 0, 4, 8, 14, 20, 24, 25, 29
TMPIECES = [[(AV, 512, 0)], [(BV, 512, 0)], [(BV + 512, 256, 0), (CV, 128, 256), (DV, 128, 384)]]
YA0, YB0, YC0, YD0 = 0, 4, 6, 10
S_FM, S_TM, S_G, S_BR, S_WO, S_UP, S_DN, NSLAB = 0, 8, 11, 27, 31, 35, 51, 67


class Sched:
    ENGS = ("pe", "act", "dve", "pool", "sp")

    def __init__(self):
        self.ops = []

    def add(self, eng, fn, reads=(), writes=(), sem=None, tag=None):
        self.ops.append((eng, fn, tuple(reads), tuple(writes), sem, tag))

    def barrier(self):
        self.ops.append(None)

    def emit(self, nc, stack):
        ops = self.ops
        n = len(ops)
        chan = [None] * n
        for i, o in enumerate(ops):
            if o is not None:
                chan[i] = ("d", o[4]) if o[4] is not None else ("e", o[0])
        W = {}
        red = [None] * n
        needs = [False] * n
        lastchan = {}
        pending = {}
        for i, o in enumerate(ops):
            if o is None:
                snap = dict(lastchan)
                pending = {e: snap for e in self.ENGS}
                W = {}
                continue
            eng, fn, reads, writes, sem, tag = o
            d = set()
            for r in reads:
                st = W.get(r)
                if st is not None:
                    d.update(st[0])
            for w in writes:
                st = W.get(w)
                if st is None:
                    W[w] = [[i], tag, [], []]
                elif tag is not None and st[1] == tag:
                    d.update(st[2])
                    st[0].append(i)
                else:
                    bd = st[0] + st[3]
                    d.update(bd)
                    W[w] = [[i], tag, bd, []]
            for r in reads:
                st = W.get(r)
                if st is None:
                    W[r] = [[], None, [], [i]]
                else:
                    st[3].append(i)
            d.discard(i)
            best = {}
            pb = pending.pop(eng, None)
            if pb:
                best.update(pb)
            for j in d:
                c = chan[j]
                if c[0] == "d":
                    j = lastchan[c]
                if c[0] == "e" and c[1] == "pe" and eng == "pe":
                    continue
                if c not in best or best[c] < j:
                    best[c] = j
            if eng == "pe":
                best.pop(("e", "pe"), None)
            red[i] = best
            for j in best.values():
                needs[j] = True
            lastchan[chan[i]] = i
        sigval = [0] * n
        cnt = {}
        for i in range(n):
            c = chan[i]
            if c is None:
                continue
            if c[0] == "d":
                cnt[c] = cnt.get(c, 0) + 16
                sigval[i] = cnt[c]
            elif needs[i]:
                cnt[c] = cnt.get(c, 0) + 1
                sigval[i] = cnt[c]
        sems = {}
        for c in cnt:
            sems[c] = stack.enter_context(nc.semaphore("s_%s_%s" % (c[0], c[1])))
        per_eng = {e: [] for e in self.ENGS}
        for i, o in enumerate(ops):
            if o is not None:
                per_eng[o[0]].append(i)
        stats = {e: [0, 0] for e in self.ENGS}

        def run(ename, e):
            waited = {}
            for i in per_eng[ename]:
                for c, j in red[i].items():
                    v = sigval[j]
                    if waited.get(c, 0) < v:
                        e.wait_ge(sems[c], v)
                        waited[c] = v
                        stats[ename][1] += 1
                ins = ops[i][1](e)
                stats[ename][0] += 1
                c = chan[i]
                if c[0] == "d":
                    ins.then_inc(sems[c], 16)
                elif needs[i]:
                    ins.then_inc(sems[c], 1)

        with nc.Block() as block:
            @block.tensor
            def _(e):
                run("pe", e)

            @block.scalar
            def _(e):
                run("act", e)

            @block.vector
            def _(e):
                run("dve", e)

            @block.gpsimd
            def _(e):
                run("pool", e)

            @block.sync
            def _(e):
                run("sp", e)
        self.stats = stats
        self.nsems = len(sems)


def alibi_slopes(n):
    return (2.0 ** (-8.0 * np.arange(1, n + 1) / n)).astype(np.float32)


def static_tables():
    bf = ml_dtypes.bfloat16
    ident = np.eye(128, dtype=np.float32)
    ones = np.ones((128, 128), np.float32)
    bd = np.zeros((128, 128), np.float32)
    bd[:64, :64] = 1.0
    bd[64:, 64:] = 1.0
    perm = np.zeros((128, 128), np.float32)
    for m in range(128):
        hb, dd = m // 64 * 64, m % 64
        w = dd % 32
        partner = dd + 16 if w < 16 else dd - 16
        perm[hb + partner, m] = 1.0
    cf32 = np.concatenate([ident, ones, bd, perm], axis=1)
    kk = np.arange(128)[:, None]
    qq = np.arange(128)[None, :]
    sd = alibi_slopes(8)
    biasd = np.zeros((128, 3, 2, 4, 128), np.float32)
    for ri, rel in enumerate((-1, 0, 1)):
        o = kk + 128 * rel - qq
        for hf in range(2):
            for j in range(4):
                h = 2 * j + hf
                biasd[:, ri, hf, j, :] = np.where(np.abs(o) <= 128, -8.0 * sd[h] * np.abs(o), NEG)
    sb = alibi_slopes(12)
    biasb = np.zeros((128, 3, 3, 2, 2, 128), np.float32)
    for g, dil in enumerate((1, 4, 16)):
        for ri, rel in enumerate((-1, 0, 1)):
            o = kk + 128 * rel - qq
            for hf in range(2):
                for j in range(2):
                    s = 2 * j + hf
                    biasb[:, g, ri, hf, j, :] = np.where(np.abs(o) <= 64, -8.0 * sb[4 * g + s] * dil * np.abs(o), NEG)
    return {"cf32": cf32, "biasd": biasd.reshape(128, 2 * 3 * 512).astype(bf),
            "biasb": biasb.reshape(128, 3 * 3 * 512).astype(bf)}


def core_tables(segs, T):
    bf = ml_dtypes.bfloat16
    NT = T // 128
    NW = T // 2048
    assert sum(segs) == T and all(L % 2048 == 0 for L in segs)
    seg_of_tok = np.zeros(T, np.int64)
    seg_start = np.zeros(T, np.int64)
    seg_len = np.zeros(T, np.int64)
    p = 0
    for si, L in enumerate(segs):
        seg_of_tok[p:p + L] = si
        seg_start[p:p + L] = p
        seg_len[p:p + L] = L
        p += L
    t_loc = np.arange(T) - seg_start
    row = (t_loc // 64).astype(np.float32)
    col = (t_loc % 64).astype(np.float32)
    freqs = (np.float32(10000.0) ** (-np.arange(16, dtype=np.float32) / np.float32(16))).astype(np.float32)
    rope = np.zeros((2, 128, T), np.float32)
    for pp in range(128):
        dd = pp % 64
        half, w = dd // 32, dd % 32
        i = w % 16
        pos = row if half == 0 else col
        ang = (pos * freqs[i]).astype(np.float32)
        rope[0, pp] = np.cos(ang)
        rope[1, pp] = np.sin(ang) * (-1.0 if w < 16 else 1.0)
    ncol = 13 * NT + NT * NW + 1
    mc = np.zeros(ncol, np.float32)
    segt = seg_of_tok[::128]
    for b in range(NT):
        for ri, rel in enumerate((-1, 0, 1)):
            k = b + rel
            ok = 0 <= k < NT and segt[k] == segt[b]
            mc[b * 3 + ri] = 0.0 if ok else NEG
    base = 3 * NT
    for g, dil in enumerate((1, 4, 16)):
        nmi = NT // dil
        for r in range(dil):
            for mi in range(nmi):
                for ri, rel in enumerate((-1, 0, 1)):
                    k = mi + rel
                    ok = 0 <= k < nmi and seg_of_tok[dil * 128 * k] == seg_of_tok[dil * 128 * mi]
                    mc[base + g * 3 * NT + (r * nmi + mi) * 3 + ri] = 0.0 if ok else NEG
    basec = 12 * NT
    for b in range(NT):
        for kw in range(NW):
            ok = seg_of_tok[2048 * kw] == segt[b]
            mc[basec + b * NW + kw] = 0.0 if ok else NEG
    mcols = np.broadcast_to(mc[None, :], (128, ncol)).copy()
    ma = np.zeros((NT, 128, 7, 128), np.float32)
    qrl = np.arange(128) // 64
    qc = np.arange(128) % 64
    c0 = np.clip(qc - 8, 0, 48)
    for b in range(NT):
        rs = seg_start[b * 128] // 64
        R = seg_len[b * 128] // 64
        qr = 2 * b + qrl
        r0 = np.clip(qr - rs - 4, 0, R - 8) + rs
        for sl in range(7):
            kt = b + sl - 3
            if kt < 0 or kt >= NT:
                continue
            kr = 2 * kt + qrl
            kcol = qc
            rowok = (kr[:, None] >= r0[None, :]) & (kr[:, None] < r0[None, :] + 8)
            colok = (kcol[:, None] >= c0[None, :]) & (kcol[:, None] < c0[None, :] + 16)
            ma[b, :, sl, :] = (rowok & colok).astype(np.float32)
    return {"rope": rope, "mcols": mcols, "maska": ma.astype(bf)}


def build_program(T, depth, phases=None, debug=False):
    assert T % 2048 == 0
    NT = T // 128
    NW = T // 2048
    NCOL = 13 * NT + NT * NW + 1
    ZCOL = NCOL - 1
    ph = phases or {"pre", "p0", "p1", "A", "B", "C", "D", "p3", "mlp", "fin"}
    nc = bass.Bass("TRN2", target_bir_lowering=False)
    S = Sched()
    st = contextlib.ExitStack()
    with st:
        def din(name, shape, dt=F32):
            return nc.dram_tensor(name, list(shape), dt, kind="ExternalInput")

        x_in = din("x", [T, D]).ap()
        norm_mix = din("norm_mix", [depth, D]).ap()
        w_in = din("w_in", [depth, D, INW]).ap()
        rel_bias_a = din("rel_bias_a", [depth, 8, 15, 31]).ap()
        q_gain = din("q_gain_c", [depth, 64]).ap()
        k_gain = din("k_gain_c", [depth, 64]).ap()
        sink_d = din("sink_d", [depth, 8]).ap()
        w_ba = din("w_branch_a", [depth, 512, D]).ap()
        w_bb = din("w_branch_b", [depth, 256, D]).ap()
        w_bc = din("w_branch_c", [depth, 512, D]).ap()
        w_bd = din("w_branch_d", [depth, 512, D]).ap()
        w_out = din("w_out", [depth, D, D]).ap()
        norm_mlp = din("norm_mlp", [depth, D]).ap()
        w_up = din("w_up", [depth, D, DFF]).ap()
        w_down = din("w_down", [depth, DFF, D]).ap()
        norm_final = din("norm_final", [D]).ap()
        rope_in = din("rope", [2, 128, T]).ap()
        mcols_in = din("mcols", [128, NCOL]).ap()
        maska_in = din("maska", [NT, 128, 7, 128], BF16).ap()
        cf32_in = din("cf32", [128, 512]).ap()
        biasd_in = din("biasd", [128, 3072], BF16).ap()
        biasb_in = din("biasb", [128, 4608], BF16).ap()
        y_out = nc.dram_tensor("y", [T, D], F32, kind="ExternalOutput").ap()
        okind = "ExternalOutput" if debug else "Internal"
        XT = nc.dram_tensor("XT", [KC, 128, T], F32, kind=okind).ap()
        HT = nc.dram_tensor("HT", [KC, 128, T], BF16).ap()
        QK = nc.dram_tensor("QK", [30, 128, T], BF16, kind=okind).ap()
        VTM_h = nc.dram_tensor("VTM", [T, 1536], BF16, kind=okind)
        VTM = VTM_h.ap()
        YT = nc.dram_tensor("YT", [14, 128, T], BF16, kind=okind).ap()
        WB = [nc.dram_tensor("WB%d" % l, [NSLAB, 128, 8192], BF16).ap() for l in range(depth)]
        EXT = nc.dram_tensor("EXT", [depth, 8, 15, 127], F32)

        def sbt(name, shape, dt):
            return st.enter_context(nc.sbuf_tensor("sb_" + name, list(shape), dt))

        ARN = 43008
        arena = sbt("arena", [128, ARN], BF16)
        wbuf = [sbt("wbuf%d" % i, [128, 8192], BF16) for i in range(3)]
        NF = 16
        Ft = [sbt("F%d" % i, [128, 512], F32) for i in range(NF)]
        NB = 8
        Bt = [sbt("B%d" % i, [128, 512], BF16) for i in range(NB)]
        cf32 = sbt("cf32", [128, 512], F32)
        identb = sbt("identb", [128, 128], BF16)
        biasd = sbt("biasd", [128, 3072], BF16)
        biasb = sbt("biasb", [128, 4608], BF16)
        mcols = sbt("mcols", [128, NCOL], F32)
        esb = sbt("esb", [64, 2, 512], F32)
        gmix = sbt("gmix", [128, 16], F32)
        gmlp = sbt("gmlp", [128, 16], F32)
        gains = sbt("gains", [128, 2], F32)
        epsb = sbt("epsb", [128, 1], F32)
        es8 = sbt("es8", [64, 8], F32)
        small = sbt("small", [128, 8], F32)
        PS = [st.enter_context(nc.psum_tensor("ps%d" % i, [128, 512], F32)) for i in range(8)]
        identf = cf32[:, 0:128]
        onesf = cf32[:, 128:256]
        bdf = cf32[:, 256:384]
        permf = cf32[:, 384:512]

        def av(off, n):
            return arena[:, off:off + n]

        def avf(off, n):
            return arena[:, off:off + 2 * n].bitcast(F32)

        SEMMAP = {"xin0": "g0", "xin1": "g1", "xs0": "g0", "xs1": "g1", "rp0": "g2", "rp1": "g3", "xc0": "g4", "xc1": "g5",
                  "aq": "g0", "aq0": "g1", "aq1": "g2", "ak": "g3", "avv": "g4", "avv0": "g4", "avv1": "g5", "am": "g6",
                  "a3h": "g0", "a3y": "g1", "a3b": "g2", "c0": "g7",
                  "xst0": "q0", "xst1": "q1", "hts": "q0", "qks0": "q1", "qks1": "q2", "vst0": "q3", "vst1": "q4", "vst2": "q5",
                  "vst3": "q6", "yst0": "q0", "yst1": "q1", "yst2": "q2", "yst3": "q3", "xn0": "q0", "xn1": "q1", "c1": "q7"}

        def dma(q, out, in_, reads, writes, sem, tag=None, slow=False):
            if sem in SEMMAP:
                sem = SEMMAP[sem]
            assert sem[0] in ("w", "cv"[0], "g", "q"), sem
            assert (q == "pool") == (sem[0] in ("q", "c")), (q, sem)
            if slow:
                S.add(q, lambda e, o=out, i=in_: e.dma_start(out=o, in_=i, allow_slow_non_contiguous=True),
                      reads, writes, sem=sem, tag=tag)
            else:
                S.add(q, lambda e, o=out, i=in_: e.dma_start(out=o, in_=i), reads, writes, sem=sem, tag=tag)

        def mm(out, lhsT, rhs, start, stop, reads, writes):
            S.add("pe", lambda e, o=out, l=lhsT, r=rhs, a=start, b=stop:
                  e.matmul(o, lhsT=l, rhs=r, start=a, stop=b, skip_group_check=True), reads, writes)

        def act(out, in_, func, reads, writes, bias=None, scale=None, accum_out=None):
            kw = {}
            if bias is not None:
                kw["bias"] = bias
            if scale is not None:
                kw["scale"] = scale
            if accum_out is not None:
                kw["accum_out"] = accum_out
            S.add("act", lambda e, o=out, i=in_, f=func, k=kw: e.activation(out=o, in_=i, func=f, **k), reads, writes)

        def tt(eng, out, in0, in1, op, reads, writes):
            S.add(eng, lambda e, o=out, a=in0, b=in1, p=op: e.tensor_tensor(out=o, in0=a, in1=b, op=p), reads, writes)

        def stt(out, in0, scalar, in1, op0, op1, reads, writes):
            S.add("dve", lambda e, o=out, a=in0, s=scalar, b=in1, p0=op0, p1=op1:
                  e.scalar_tensor_tensor(out=o, in0=a, scalar=s, in1=b, op0=p0, op1=p1), reads, writes)

        def tsc(eng, out, in0, s1, op0, reads, writes, s2=None, op1=None):
            if op1 is None:
                S.add(eng, lambda e, o=out, a=in0, s=s1, p=op0: e.tensor_scalar(out=o, in0=a, scalar1=s, scalar2=None, op0=p),
                      reads, writes)
            else:
                S.add(eng, lambda e, o=out, a=in0, s=s1, t=s2, p=op0, q=op1:
                      e.tensor_scalar(out=o, in0=a, scalar1=s, scalar2=t, op0=p, op1=q), reads, writes)

        def copy(eng, out, in_, reads, writes):
            if eng == "act":
                act(out, in_, AF.Copy, reads, writes)
            else:
                S.add(eng, lambda e, o=out, i=in_: e.tensor_copy(out=o, in_=i), reads, writes)

        def recip(out, in_, reads, writes):
            S.add("dve", lambda e, o=out, i=in_: e.reciprocal(out=o, in_=i), reads, writes)

        def memset(eng, ap, val, writes):
            S.add(eng, lambda e, a=ap, v=val: e.memset(a, v), (), writes)

        class Pipe:
            def __init__(self):
                self.pend = None

            def unit(self, s1, s2):
                s1()
                if self.pend is not None:
                    self.pend()
                self.pend = s2

            def flush(self):
                if self.pend is not None:
                    self.pend()
                    self.pend = None

        class Pipe2:
            def __init__(self):
                self.q = []

            def push(self, a, b):
                if len(self.q) >= 2:
                    self.q[-2][1]()
                if len(self.q) >= 1:
                    self.q[-1][0]()
                self.q.append((a, b))
                self.q = self.q[-2:]

            def flush(self):
                if len(self.q) >= 2:
                    self.q[-2][1]()
                if len(self.q) >= 1:
                    self.q[-1][0]()
                    self.q[-1][1]()
                self.q = []

        rot = {}

        def nxt(key, n):
            v = rot.get(key, 0)
            rot[key] = v + 1
            return v % n

        def F(i):
            return Ft[i][:], ("F", i)

        def B(i):
            return Bt[i][:], ("B", i)

        def ps(i):
            return PS[i], ("ps", i)

        alt = [0]

        def evac_eng():
            alt[0] += 1
            return "act" if alt[0] % 2 else "dve"

        dma("sp", cf32[:], cf32_in, [], ["cf32"], "c0")
        dma("sp", biasd[:], biasd_in, [], ["biasd"], "c0")
        dma("sp", biasb[:], biasb_in, [], ["biasb"], "c0")
        dma("sp", mcols[:], mcols_in, [], ["mcols"], "c0")
        copy("dve", identb[:], identf, ["cf32"], ["identb"])
        memset("dve", epsb[:], EPS, ["epsb"])
        memset("pool", arena[:], 0.0, ["arena0"])
        for i in range(NF):
            memset("dve", Ft[i][:], 0.0, [("F", i)])
        if "A" in ph:
            for l in range(depth):
                for h in range(8):
                    src = Ft[0][0:15, 0:127]
                    dma("pool", bass.AP(tensor=EXT, offset=(l * 8 + h) * 15 * 127, ap=[[127, 15], [1, 127]]), src,
                        [("F", 0)], [("ext", l)], "c1", tag=("extz", l))
        S.barrier()

        cvn = [0]

        def wsrc(w2d, c0, ncols):
            return w2d[:, c0:c0 + ncols].rearrange("(k p) n -> p k n", p=128)

        def convert(l, slab, pieces):
            k = cvn[0] % 4
            cvn[0] += 1
            tag = ("cv", cvn[0])
            for dst, src in pieces:
                dma("pool", dst, src, [], [("wb", l, slab), ("cvslot", k)], "cv%d" % k, tag=tag)

        def slabv(l, s, kk, nn):
            return WB[l][s][:, 0:kk * nn].rearrange("p (k n) -> p k n", k=kk)

        if "pre" in ph:
            for l in range(depth):
                for s in range(8):
                    chunks = list(range(4 * s, min(4 * s + 4, 30)))
                    pieces = []
                    i = 0
                    while i < len(chunks):
                        j = i
                        while j + 1 < len(chunks) and FMCOLS[chunks[j + 1]] == FMCOLS[chunks[j]] + 128:
                            j += 1
                        n = j - i + 1
                        pieces.append((slabv(l, S_FM + s, 16, 512)[:, :, i * 128:(i + n) * 128],
                                       wsrc(w_in[l], FMCOLS[chunks[i]], n * 128)))
                        i = j + 1
                    if s == 7:
                        pieces.append((slabv(l, S_FM + s, 16, 512)[:, :, 256:384], wsrc(w_in[l], FMCOLS[28], 128)))
                        pieces.append((slabv(l, S_FM + s, 16, 512)[:, :, 384:512], wsrc(w_in[l], FMCOLS[29], 128)))
                    convert(l, S_FM + s, pieces)
                for s in range(3):
                    convert(l, S_TM + s, [(slabv(l, S_TM + s, 16, 512)[:, :, d0:d0 + n], wsrc(w_in[l], c0, n))
                                          for (c0, n, d0) in TMPIECES[s]])
                for j in range(16):
                    convert(l, S_G + j, [(slabv(l, S_G + j, 16, 512)[:, :, i * 128:(i + 1) * 128],
                                          wsrc(w_in[l], GL + i * 2048 + j * 128, 128)) for i in range(4)])
                for jq in range(4):
                    bv = slabv(l, S_BR + jq, 14, 512)
                    convert(l, S_BR + jq, [(bv[:, 0:4, :], wsrc(w_ba[l], jq * 512, 512)),
                                           (bv[:, 4:6, :], wsrc(w_bb[l], jq * 512, 512)),
                                           (bv[:, 6:10, :], wsrc(w_bc[l], jq * 512, 512)),
                                           (bv[:, 10:14, :], wsrc(w_bd[l], jq * 512, 512))])
                for s in range(4):
                    convert(l, S_WO + s, [(slabv(l, S_WO + s, 16, 512), wsrc(w_out[l], s * 512, 512))])
                for s in range(16):
                    convert(l, S_UP + s, [(slabv(l, S_UP + s, 16, 512), wsrc(w_up[l], s * 512, 512))])
                for j in range(16):
                    convert(l, S_DN + j, [(slabv(l, S_DN + j, 64, 128), wsrc(w_down[l], j * 128, 128))])

        def load_slab(l, slab, nel=8192):
            k = nxt("w", 3)
            dma("sp", wbuf[k][:, 0:nel], WB[l][slab][:, 0:nel], [("wb", l, slab)], [("wbuf", k)], "w%d" % k)
            return k

        if "p0" in ph:
            for t in range(NT):
                b2 = t % 2
                xin = avf(b2 * 4096, 2048)
                xst = avf(8192 + b2 * 4096, 2048).rearrange("p (k n) -> p k n", k=16)
                dma("sp", xin, x_in[t * 128:(t + 1) * 128, :], [], [("xin", b2)], "xin%d" % b2)
                for q in range(4):
                    pi = (t % 2) * 4 + q
                    for j in range(4):
                        kc = 4 * q + j
                        S.add("pe", lambda e, o=PS[pi][:, j * 128:(j + 1) * 128], i=xin[:, kc * 128:(kc + 1) * 128]:
                              e.transpose(o, i, identf), [("xin", b2), "cf32"], [("ps", pi)])
                    copy(evac_eng(), xst[:, 4 * q:4 * q + 4, :], PS[pi][:].rearrange("p (k n) -> p k n", k=4),
                         [("ps", pi)], [("xst", b2, q)])
                dma("pool", XT[:, :, t * 128:(t + 1) * 128].rearrange("k p n -> p k n"), xst,
                    [("xst", b2, q) for q in range(4)], [("XT", t // 4)], "xst%d" % b2)
            S.barrier()

        def norm_fm(tok0, gt, gres, hview, hres):
            rstd, rstd_r = F(4)
            sd, sd_r = F(5)
            psn, psn_r = ps(7)
            for pas in range(2):
                for grp in range(8):
                    xb = nxt("xs", 2)
                    xs = [Ft[2 * xb][:], Ft[2 * xb + 1][:]]
                    for u in range(2):
                        kc = 2 * grp + u
                        dma("sp", xs[u], XT[kc, :, tok0:tok0 + 512], [("XT", tok0 // 512, kc)], [("xs", xb)],
                            "xs%d" % xb, tag=("xs", rot["xs"]))
                    for u in range(2):
                        kc = 2 * grp + u
                        if pas == 0:
                            sq, sq_r = F(6 + nxt("sq", 2))
                            tt("pool", sq, xs[u], xs[u], ALU.mult, [("xs", xb)], [sq_r])
                            mm(psn[:], onesf, sq, kc == 0, kc == 15, [sq_r, "cf32"], [psn_r])
                        else:
                            stt(hview[:, kc, :], xs[u], gt[:, kc:kc + 1], rstd, ALU.mult, ALU.mult,
                                [("xs", xb), gres, rstd_r], [hres])
                if pas == 0:
                    act(sd, psn[:], AF.Sqrt, [psn_r, "epsb"], [sd_r], bias=epsb[:], scale=1.0 / D)
                    recip(rstd, sd, [sd_r], [rstd_r])

        def load_gvec(dst, src_row, res):
            dma("sp", dst[:], src_row.rearrange("(k p) -> p k", p=128), [], [res], "c0", slow=True)

        for l in range(depth):
            load_gvec(gmix, norm_mix[l], "gmix")
            load_gvec(gmlp, norm_mlp[l], "gmlp")
            for hh in range(2):
                dma("sp", gains[hh * 64:(hh + 1) * 64, 0:1], q_gain[l].rearrange("(d o) -> d o", o=1), [], ["gains"], "c0",
                    tag=("gn", l), slow=True)
                dma("sp", gains[hh * 64:(hh + 1) * 64, 1:2], k_gain[l].rearrange("(d o) -> d o", o=1), [], ["gains"], "c0",
                    tag=("gn", l), slow=True)

            if "p1" in ph:
                hT = av(0, 16384).rearrange("p (k n) -> p k n", k=16)
                for blk in range(T // 1024):
                    t0 = blk * 1024
                    for sb in range(2):
                        norm_fm(t0 + sb * 512, gmix, "gmix", hT[:, :, sb * 512:(sb + 1) * 512], ("hT", sb))
                    dma("pool", HT[:, :, t0:t0 + 1024].rearrange("k p n -> p k n"), hT, [("hT", 0), ("hT", 1)],
                        [("HT", blk)], "hts")
                    for sb in range(2):
                        dma("sp", Ft[8 + sb][:], rope_in[0, :, t0 + sb * 512:t0 + (sb + 1) * 512], [], [("rope", sb)], "rp%d" % sb,
                            tag=("rp", blk))
                        dma("sp", Ft[10 + sb][:], rope_in[1, :, t0 + sb * 512:t0 + (sb + 1) * 512], [], [("rope", sb)], "rp%d" % sb,
                            tag=("rp", blk))
                    p2 = Pipe2()
                    for s in range(8):
                        k = load_slab(l, S_FM + s)
                        wv = wbuf[k][:].rearrange("p (k n) -> p k n", k=16)
                        for ci in range(4):
                            ch = 4 * s + ci
                            if ch >= 30:
                                break
                            sti = nxt("qkst", 2)
                            stg = [Bt[2 * sti][:], Bt[2 * sti + 1][:]]
                            stag = ("qks", rot["qkst"])
                            for sb in range(2):
                                pi = nxt("p1ps", 4)
                                pp, pr = ps(pi)
                                for kc in range(16):
                                    mm(pp[:], wv[:, kc, ci * 128:(ci + 1) * 128], hT[:, kc, sb * 512:(sb + 1) * 512],
                                       kc == 0, kc == 15, [("wbuf", k), ("hT", sb)], [pr])
                                sres = ("B", 2 * sti + sb)
                                isc = QC0 <= ch <= KC0

                                def stA(ch=ch, sb=sb, pp=pp, pr=pr, stg=stg, sres=sres, isc=isc):
                                    if not isc:
                                        copy(evac_eng(), stg[sb], pp[:], [pr], [sres])
                                        return
                                    gcol = 0 if ch < KC0 else 1
                                    qf, qf_r = F(12)
                                    sq, sq_r = F(13)
                                    qn, qn_r = F(14)
                                    sdc, sdc_r = F(15)
                                    pb, pb_r = ps(4 + nxt("cps", 2))
                                    copy("act", qf, pp[:], [pr], [qf_r])
                                    tt("pool", sq, qf, qf, ALU.mult, [qf_r], [sq_r])
                                    mm(pb[:], bdf, sq, True, True, [sq_r, "cf32"], [pb_r])
                                    act(sdc, pb[:], AF.Sqrt, [pb_r, "epsb"], [sdc_r], bias=epsb[:], scale=1.0 / 64)
                                    recip(sdc, sdc, [sdc_r], [sdc_r])
                                    stt(qn, qf, gains[:, gcol:gcol + 1], sdc, ALU.mult, ALU.mult, [qf_r, "gains", sdc_r], [qn_r])

                                def stB(ch=ch, sb=sb, stg=stg, sres=sres, isc=isc, sti=sti, stag=stag, t0=t0):
                                    if isc:
                                        qf, qf_r = F(12)
                                        sq, sq_r = F(13)
                                        qn, qn_r = F(14)
                                        pb2, pb2_r = ps(4 + nxt("cps", 2))
                                        mm(pb2[:], permf, qn, True, True, [qn_r, "cf32"], [pb2_r])
                                        tt("dve", sq, pb2[:], Ft[10 + sb][:], ALU.mult, [pb2_r, ("rope", sb)], [sq_r])
                                        tt("dve", qf, qn, Ft[8 + sb][:], ALU.mult, [qn_r, ("rope", sb)], [qf_r])
                                        tt("dve", stg[sb], qf, sq, ALU.add, [qf_r, sq_r], [sres])
                                    if sb == 1:
                                        for sb_ in range(2):
                                            dma("pool", QK[ch, :, t0 + sb_ * 512:t0 + (sb_ + 1) * 512], stg[sb_], [("B", 2 * sti + sb_)],
                                                [("QK", ch, blk)], "qks%d" % sti, tag=stag)
                                p2.push(stA, stB)
                    p2.flush()
                    for s in range(0 if os.environ.get('P1_NOTM') else 3):
                        k = load_slab(l, S_TM + s)
                        wv = wbuf[k][:].rearrange("p (k n) -> p k n", k=16)
                        for tti in range(8):
                            pi = nxt("p1ps", 4)
                            pp, pr = ps(pi)
                            for kc in range(16):
                                mm(pp[:], hT[:, kc, tti * 128:(tti + 1) * 128], wv[:, kc, :], kc == 0, kc == 15,
                                   [("wbuf", k), ("hT", tti // 4)], [pr])
                            vi = 4 + nxt("vst", 4)
                            copy(evac_eng(), Bt[vi][:], pp[:], [pr], [("B", vi)])
                            dma("pool", VTM[t0 + tti * 128:t0 + (tti + 1) * 128, s * 512:(s + 1) * 512], Bt[vi][:], [("B", vi)],
                                [("VTM", blk)], "vst%d" % (vi - 4))
                S.barrier()

            def attn_norm_store(po, po_r, ych0, tok0, ntok, sink_g=None):
                dn, dn_r = F(nxt("dn", 2))
                yi = nxt("yst", 4)
                yst = Bt[yi][0:64, :]
                if sink_g is not None:
                    copy("dve", dn[0:64, :], po[64:128, :], [po_r], [dn_r])
                    tt("dve", dn[0:64, :], dn[0:64, :], esb[:, sink_g, :], ALU.add, [dn_r, "esb"], [dn_r])
                    recip(dn[0:64, :], dn[0:64, :], [dn_r], [dn_r])
                else:
                    recip(dn[0:64, :], po[64:128, :], [po_r], [dn_r])
                tt("dve", yst, po[0:64, :], dn[0:64, :], ALU.mult, [po_r, dn_r], [("B", yi)])
                for half in range(2):
                    dma("pool", YT[ych0:ych0 + 2, half * 64:(half + 1) * 64, tok0:tok0 + 128].rearrange("c p n -> p c n"),
                        yst[:, half * 256:(half + 1) * 256].rearrange("p (c n) -> p c n", c=2), [("B", yi)],
                        [("YT", ych0, tok0 // 128)], "yst%d" % yi, tag=("yst", rot["yst"]))

            if "D" in ph:
                dma("sp", es8[:], sink_d[l:l + 1, :].to_broadcast([64, 8]), [], ["es8"], "c0", slow=True)
                act(es8[:], es8[:], AF.Exp, ["es8"], ["es8"])
                for g in range(2):
                    for j, h in enumerate((4 * g, 4 * g + 2, 4 * g + 1, 4 * g + 3)):
                        copy("dve", esb[:, g, j * 128:(j + 1) * 128], es8[:, h:h + 1].to_broadcast([64, 128]), ["es8"], ["esb"])
                qD = av(0, 4096).rearrange("p (c n) -> p c n", c=4)
                kD = av(4096, 2560).rearrange("p (g n) -> p g n", g=2)
                vD = av(6656, 2560).rearrange("p (t g n) -> p t g n", t=10, g=2)
                memset("dve", vD[:, :, :, 64:128], 1.0, ["vD1"])
                for sbk in range(T // 1024):
                    t0 = sbk * 1024
                    lo = max(t0 - 128, 0)
                    hi = min(t0 + 1152, T)
                    off = lo - (t0 - 128)
                    dma("sp", qD, QK[QD0:QD0 + 4, :, t0:t0 + 1024].rearrange("c p n -> p c n"), [("QK", QD0 + c, sbk) for c in range(4)],
                        ["qD"], "aq")
                    if off > 0:
                        memset("dve", kD[:, :, 0:off], 0.0, ["kD"])
                        memset("dve", vD[:, 0:off // 128, :, 0:64], 0.0, ["vD"])
                    if off + hi - lo < 1280:
                        memset("dve", kD[:, :, off + hi - lo:1280], 0.0, ["kD"])
                        memset("dve", vD[:, (off + hi - lo) // 128:10, :, 0:64], 0.0, ["vD"])
                    for g in range(2):
                        for half in range(2):
                            dma("sp", kD[half * 64:(half + 1) * 64, g, off:off + hi - lo], QK[KD0, g * 64:(g + 1) * 64, lo:hi],
                                [("QK", KD0, b) for b in range(T // 1024)], ["kD"], "ak", tag=("kD", sbk))
                        dma("sp", vD[:, off // 128:(off + hi - lo) // 128, g, 0:64],
                            VTM[lo:hi, 1408 + g * 64:1408 + (g + 1) * 64].rearrange("(t p) d -> p t d", p=128),
                            [("VTM", b) for b in range(T // 1024)], ["vD"], "avv", tag=("vD", sbk))
                    pipe = Pipe()
                    for bl in range(8):
                        b = sbk * 8 + bl
                        pob = nxt("po", 2)
                        pos = [ps(4 + 2 * pob + g) for g in range(2)]
                        for ri in range(3):
                            kt = bl + ri
                            sb2 = 2 * nxt("pss", 2)
                            pb2 = 2 * nxt("pb", 2)

                            def s1(bl=bl, b=b, ri=ri, kt=kt, sb2=sb2, pb2=pb2):
                                for hf in range(2):
                                    pss, pss_r = ps(sb2 + hf)
                                    mm(pss[:], identb[:], biasd[:, (ri * 2 + hf) * 512:(ri * 2 + hf + 1) * 512], True, False,
                                       ["identb", "biasd"], [pss_r])
                                    for g in range(2):
                                        mm(pss[:, g * 256:(g + 1) * 256].rearrange("p (c n) -> p c n", c=2),
                                           kD[hf * 64:(hf + 1) * 64, g, kt * 128:(kt + 1) * 128],
                                           qD[hf * 64:(hf + 1) * 64, 2 * g:2 * g + 2, bl * 128:(bl + 1) * 128], False, g == 1,
                                           ["kD", "qD"], [pss_r])
                                    act(Bt[4 + pb2 + hf][:], pss[:], AF.Exp, [pss_r, "mcols"], [("B", 4 + pb2 + hf)],
                                        bias=mcols[:, b * 3 + ri:b * 3 + ri + 1], scale=0.125)

                            def s2(b=b, ri=ri, kt=kt, pb2=pb2, pos=pos):
                                for g in range(2):
                                    po, po_r = pos[g]
                                    for hf in range(2):
                                        mm(po[:, hf * 256:(hf + 1) * 256], vD[:, kt, g, :], Bt[4 + pb2 + hf][:, g * 256:(g + 1) * 256],
                                           ri == 0 and hf == 0, ri == 2 and hf == 1, ["vD", "vD1", ("B", 4 + pb2 + hf)], [po_r])
                                if ri == 2:
                                    for g in range(2):
                                        attn_norm_store(pos[g][0], pos[g][1], YD0 + 2 * g, b * 128, 128, sink_g=g)
                            pipe.unit(s1, s2)
                    pipe.flush()
                S.barrier()

            if "C" in ph:
                kCt = av(0, 2 * T).rearrange("p (g n) -> p g n", g=2)
                vC = av(2 * T, NT * 256).rearrange("p (t g n) -> p t g n", t=NT, g=2)
                memset("dve", vC[:, :, :, 64:128], 1.0, ["vC1"])
                for g in range(2):
                    for half in range(2):
                        dma("sp", kCt[half * 64:(half + 1) * 64, g, :], QK[KC0, g * 64:(g + 1) * 64, :], [("QK", KC0, b) for b in range(T // 1024)],
                            ["kC"], "ak", tag="kC")
                    for tq in range(NT // 16):
                        dma("sp", vC[:, tq * 16:(tq + 1) * 16, g, 0:64],
                            VTM[tq * 2048:(tq + 1) * 2048, 1280 + g * 64:1280 + (g + 1) * 64].rearrange("(t p) d -> p t d", p=128),
                            [("VTM", b) for b in range(T // 1024)], ["vC"], "avv", tag="vC")
                pipe = Pipe()
                for b in range(NT):
                    qi = nxt("qC", 2)
                    qCt = av(2 * T + NT * 256 + qi * 512, 512).rearrange("p (c n) -> p c n", c=4)
                    dma("sp", qCt, QK[QC0:QC0 + 4, :, b * 128:(b + 1) * 128].rearrange("c p n -> p c n"),
                        [("QK", QC0 + c, b // 8) for c in range(4)], [("qC", qi)], "aq%d" % qi)
                    pob = nxt("po", 2)
                    pos = [ps(4 + 2 * pob + g) for g in range(2)]
                    for kt in range(NT):
                        sb2 = 2 * nxt("pss", 2)
                        pb2 = 2 * nxt("pb", 2)
                        mcol = 12 * NT + b * NW + kt // 16

                        def s1(kt=kt, sb2=sb2, pb2=pb2, mcol=mcol, qCt=qCt, qi=qi):
                            for hf in range(2):
                                pss, pss_r = ps(sb2 + hf)
                                for g in range(2):
                                    mm(pss[:, g * 256:(g + 1) * 256].rearrange("p (c n) -> p c n", c=2),
                                       kCt[hf * 64:(hf + 1) * 64, g, kt * 128:(kt + 1) * 128],
                                       qCt[hf * 64:(hf + 1) * 64, 2 * g:2 * g + 2, :], g == 0, g == 1, ["kC", ("qC", qi)], [pss_r])
                                act(Bt[4 + pb2 + hf][:], pss[:], AF.Exp, [pss_r, "mcols"], [("B", 4 + pb2 + hf)],
                                    bias=mcols[:, mcol:mcol + 1], scale=0.125)

                        def s2(b=b, kt=kt, pb2=pb2, pos=pos):
                            for g in range(2):
                                po, po_r = pos[g]
                                for hf in range(2):
                                    mm(po[:, hf * 256:(hf + 1) * 256], vC[:, kt, g, :], Bt[4 + pb2 + hf][:, g * 256:(g + 1) * 256],
                                       kt == 0 and hf == 0, kt == NT - 1 and hf == 1, ["vC", "vC1", ("B", 4 + pb2 + hf)], [po_r])
                            if kt == NT - 1:
                                for g in range(2):
                                    attn_norm_store(pos[g][0], pos[g][1], YC0 + 2 * g, b * 128, 128)
                        pipe.unit(s1, s2)
                pipe.flush()
                S.barrier()

            if "A" in ph:
                rbt, rbt_r = F(12)
                dma("sp", rbt[0:8, 0:465], rel_bias_a[l].rearrange("h r c -> h (r c)"), [], [rbt_r], "c0")
                tsc("dve", rbt[0:8, 0:465], rbt[0:8, 0:465], 8.0, ALU.mult, [rbt_r], [rbt_r])
                dma("pool", bass.AP(tensor=EXT, offset=l * 8 * 15 * 127 + 48, ap=[[15 * 127, 8], [127, 15], [1, 31]]),
                    rbt[0:8, 0:465].rearrange("h (r c) -> h r c", r=15), [rbt_r, ("ext", l)], [("ext2", l)], "c1")
                TOE = av(0, 7680)[0:64, :].rearrange("p (m n) -> p m n", m=120)
                dma("pool", TOE, bass.AP(tensor=EXT, offset=l * 8 * 15 * 127, ap=[[1, 64], [127, 120], [1, 64]]),
                    [("ext2", l)], ["TOE"], "c1")
                RBA = av(32768, 7168).rearrange("p (s f c n) -> p s f c n", s=7, f=2, c=4)
                memset("dve", av(32768, 7168), 0.0, ["RBA"])
                TOE4 = av(0, 7680)[0:64, :].rearrange("p (h r n) -> p h r n", h=8, r=15)
                for sl in range(7):
                    for krl in range(2):
                        for qrl in range(2):
                            dri = 2 * (sl - 3) + krl - qrl + 7
                            if 0 <= dri <= 14:
                                for hf in range(2):
                                    copy("dve", RBA[krl * 64:(krl + 1) * 64, sl, hf, :, qrl * 64:(qrl + 1) * 64],
                                         TOE4[:, hf::2, dri, ::-1], ["TOE"], ["RBA"])
                S.barrier()
                qA = av(0, 4096).rearrange("p (c n) -> p c n", c=4)
                kA = av(4096, 7168).rearrange("p (c n) -> p c n", c=4)
                vA = av(11264, 14336).rearrange("p (t h n) -> p t h n", t=14, h=8)
                mA = av(25600, 7168).rearrange("p (b s n) -> p b s n", b=8, s=7)
                memset("dve", vA[:, :, :, 64:128], 1.0, ["vA1"])
                for sbk in range(T // 1024):
                    t0 = sbk * 1024
                    lo = max(t0 - 384, 0)
                    hi = min(t0 + 1024 + 384, T)
                    off = lo - (t0 - 384)
                    dma("sp", qA, QK[QA0:QA0 + 4, :, t0:t0 + 1024].rearrange("c p n -> p c n"), [("QK", QA0 + c, sbk) for c in range(4)],
                        ["qA"], "aq")
                    if off > 0:
                        memset("dve", kA[:, :, 0:off], 0.0, ["kA"])
                        memset("dve", vA[:, 0:off // 128, :, 0:64], 0.0, ["vA"])
                    if off + hi - lo < 1792:
                        memset("dve", kA[:, :, off + hi - lo:1792], 0.0, ["kA"])
                        memset("dve", vA[:, (off + hi - lo) // 128:14, :, 0:64], 0.0, ["vA"])
                    dma("sp", kA[:, :, off:off + hi - lo], QK[KA0:KA0 + 4, :, lo:hi].rearrange("c p n -> p c n"),
                        [("QK", KA0 + c, b) for c in range(4) for b in range(T // 1024)], ["kA"], "ak")
                    for tq in range(off // 128, (off + hi - lo) // 128):
                        tk = (t0 - 384) // 128 + tq
                        dma("sp", vA[:, tq, :, 0:64], VTM[tk * 128:(tk + 1) * 128, 0:512].rearrange("p (h d) -> p h d", h=8),
                            [("VTM", b) for b in range(T // 1024)], ["vA"], "avv", tag=("vA", sbk))
                    dma("sp", mA, maska_in[sbk * 8:(sbk + 1) * 8].rearrange("b p s n -> p b s n"), [], ["mA"], "am")
                    pipe = Pipe()
                    for bl in range(8):
                        b = sbk * 8 + bl
                        pob = nxt("po", 2)
                        pos = [ps(4 + 2 * pob + hf) for hf in range(2)]
                        sls = [sl for sl in range(7) if not ((sl == 0 and b % 16 != 15) or (sl == 6 and b % 16 != 0))]
                        for sl in sls:
                            kt = bl + sl
                            sb2 = 2 * nxt("pss", 2)
                            pb2 = 2 * nxt("pb", 2)

                            def s1(bl=bl, sl=sl, kt=kt, sb2=sb2, pb2=pb2):
                                for hf in range(2):
                                    pss, pss_r = ps(sb2 + hf)
                                    mm(pss[:].rearrange("p (c n) -> p c n", c=4), identb[:], RBA[:, sl, hf, :, :], True, False, ["identb", "RBA"], [pss_r])
                                    for c in range(4):
                                        mm(pss[:, c * 128:(c + 1) * 128], kA[hf * 64:(hf + 1) * 64, c, kt * 128:(kt + 1) * 128],
                                           qA[hf * 64:(hf + 1) * 64, c, bl * 128:(bl + 1) * 128], False, c == 3, ["kA", "qA"], [pss_r])
                                    pt = Bt[4 + pb2 + hf][:]
                                    pr_ = ("B", 4 + pb2 + hf)
                                    act(pt, pss[:], AF.Exp, [pss_r], [pr_], scale=0.125)
                                    tt("dve", pt.rearrange("p (h n) -> p h n", h=4), pt.rearrange("p (h n) -> p h n", h=4),
                                       mA[:, bl, sl, :].unsqueeze(1).to_broadcast([128, 4, 128]), ALU.mult, [pr_, "mA"], [pr_])

                            def s2(b=b, sl=sl, kt=kt, pb2=pb2, pos=pos, sl0=sls[0], sl1=sls[-1]):
                                for hf in range(2):
                                    po, po_r = pos[hf]
                                    for c in range(4):
                                        mm(po[:, c * 128:(c + 1) * 128], vA[:, kt, 2 * c + hf, :], Bt[4 + pb2 + hf][:, c * 128:(c + 1) * 128],
                                           sl == sl0 and c == 0, sl == sl1 and c == 3, ["vA", "vA1", ("B", 4 + pb2 + hf)], [po_r])
                                if sl == sl1:
                                    for hf in range(2):
                                        po, po_r = pos[hf]
                                        dn, dn_r = F(nxt("dn", 2))
                                        yi = nxt("yst", 4)
                                        yst = Bt[yi][0:64, :]
                                        recip(dn[0:64, :], po[64:128, :], [po_r], [dn_r])
                                        tt("dve", yst, po[0:64, :], dn[0:64, :], ALU.mult, [po_r, dn_r], [("B", yi)])
                                        dma("pool", YT[YA0:YA0 + 4, hf * 64:(hf + 1) * 64, b * 128:(b + 1) * 128].rearrange("c p n -> p c n"),
                                            yst.rearrange("p (c n) -> p c n", c=4), [("B", yi)], [("YT", YA0, b)], "yst%d" % yi)
                            pipe.unit(s1, s2)
                    pipe.flush()
                S.barrier()

            if "B" in ph:
                acc = avf(0, 8192).rearrange("p (s n) -> p s n", s=4)
                kB = av(16384, 12288).rearrange("p (c n) -> p c n", c=2)
                qB = av(28672, 4096).rearrange("p (c n) -> p c n", c=2)
                vB = av(32768, 9216).rearrange("p (t s n) -> p t s n", t=18, s=4)
                memset("dve", vB[:, :, :, 64:128], 1.0, ["vB1"])
                for w in range(NW):
                    w0 = w * 2048
                    for g, dil in enumerate((1, 4, 16)):
                        halo = 128 * dil
                        lo = max(w0 - halo, 0)
                        hi = min(w0 + 2048 + halo, T)
                        off = lo - (w0 - halo)
                        dma("sp", qB, QK[QB0 + 2 * g:QB0 + 2 * g + 2, :, w0:w0 + 2048].rearrange("c p n -> p c n"),
                            [("QK", QB0 + 2 * g + c, 2 * w + i) for c in range(2) for i in range(2)], ["qB"], "aq")
                        if off > 0:
                            memset("dve", kB[:, :, 0:off], 0.0, ["kB"])
                        if off + hi - lo < 2048 + 2 * halo:
                            memset("dve", kB[:, :, off + hi - lo:2048 + 2 * halo], 0.0, ["kB"])
                        dma("sp", kB[:, :, off:off + hi - lo], QK[KB0 + 2 * g:KB0 + 2 * g + 2, :, lo:hi].rearrange("c p n -> p c n"),
                            [("QK", KB0 + 2 * g + c, b) for c in range(2) for b in range(T // 1024)], ["kB"], "ak")
                        nrun = 16 // dil
                        pipeB = Pipe()
                        for r in range(dil):
                            mi0 = w0 // (128 * dil)
                            nmi = NT // dil
                            vb0 = (nxt("vBh", 2) * 9) if nrun + 2 <= 9 else 0
                            for tq in range(nrun + 2):
                                mi = mi0 - 1 + tq
                                if mi < 0 or mi >= nmi:
                                    S.add("dve", lambda e, a=vB[:, vb0 + tq, :, 0:64]: e.memset(a, 0.0), (), [("vB", vb0)], tag=("vB", w, g, r))
                                    continue
                                tokb = dil * 128 * mi + r
                                src = bass.AP(tensor=VTM_h, offset=tokb * 1536 + 512 + 256 * g,
                                              ap=[[1536 * dil, 128], [64, 4], [1, 64]])
                                dma("sp", vB[:, vb0 + tq, :, 0:64], src, [("VTM", b) for b in range(T // 1024)], [("vB", vb0)],
                                    "avv%d" % (vb0 // 9), tag=("vB", w, g, r))
                            for ml in range(nrun):
                                mi = mi0 + ml
                                qoff = r + dil * 128 * ml
                                pobk = 4 + nxt("poB", 4)
                                for ri in range(3):
                                    koff = halo + r + dil * 128 * (ml + ri - 1)
                                    sb2 = 2 * nxt("pss", 2)
                                    pi = nxt("pb", 4)
                                    mcol = 3 * NT + g * 3 * NT + (r * nmi + mi) * 3 + ri
                                    vt = vb0 + ml + ri

                                    def s1(g=g, dil=dil, ri=ri, koff=koff, qoff=qoff, sb2=sb2, pi=pi, mcol=mcol):
                                        for hf in range(2):
                                            pss, pss_r = ps(sb2 + hf)
                                            bo = ((g * 3 + ri) * 2 + hf) * 256
                                            mm(pss[:, 0:256], identb[:], biasb[:, bo:bo + 256], True, False, ["identb", "biasb"], [pss_r])
                                            for j in range(2):
                                                mm(pss[:, j * 128:(j + 1) * 128],
                                                   kB[hf * 64:(hf + 1) * 64, j, koff:koff + 127 * dil + 1:dil],
                                                   qB[hf * 64:(hf + 1) * 64, j, qoff:qoff + 127 * dil + 1:dil], False, j == 1, ["kB", "qB"], [pss_r])
                                            act(Bt[4 + pi][:, hf * 256:(hf + 1) * 256], pss[:, 0:256], AF.Exp, [pss_r, "mcols"], [("B", 4 + pi)],
                                                bias=mcols[:, mcol:mcol + 1], scale=0.125)

                                    def s2(g=g, dil=dil, ri=ri, qoff=qoff, pi=pi, vt=vt, vb0=vb0, pobk=pobk):
                                        po, po_r = ps(pobk)
                                        for s_ in range(4):
                                            pj = (s_ % 2) * 2 + s_ // 2
                                            mm(po[:, s_ * 128:(s_ + 1) * 128], vB[:, vt, s_, :], Bt[4 + pi][:, pj * 128:(pj + 1) * 128],
                                               ri == 0 and s_ == 0, ri == 2 and s_ == 3, [("vB", vb0), "vB1", ("B", 4 + pi)], [po_r])
                                        if ri == 2:
                                            accv = acc[:, :, qoff:qoff + 127 * dil + 1:dil]
                                            pov = po[:].rearrange("p (s n) -> p s n", s=4)
                                            if g == 0:
                                                copy("dve", accv, pov, [po_r], ["acc"])
                                            else:
                                                tt("dve", accv, pov, accv, ALU.add, [po_r, "acc"], ["acc"])
                                    pipeB.unit(s1, s2)
                        pipeB.flush()
                    for s_ in range(4):
                        for hfw in range(4):
                            dn, dn_r = F(nxt("dn", 2))
                            yi = nxt("yst", 4)
                            yst = Bt[yi][0:64, :]
                            recip(dn[0:64, :], acc[64:128, s_, hfw * 512:(hfw + 1) * 512], ["acc"], [dn_r])
                            tt("dve", yst, acc[0:64, s_, hfw * 512:(hfw + 1) * 512], dn[0:64, :], ALU.mult, ["acc", dn_r], [("B", yi)])
                            dma("pool", YT[YB0 + s_ // 2, (s_ % 2) * 64:(s_ % 2 + 1) * 64, w0 + hfw * 512:w0 + (hfw + 1) * 512], yst, [("B", yi)],
                                [("YT", YB0 + s_ // 2, w)], "yst%d" % yi)
                S.barrier()

            if "p3" in ph:
                hT3 = av(0, 8192).rearrange("p (k n) -> p k n", k=16)
                yT3 = av(8192, 7168).rearrange("p (k n) -> p k n", k=14)
                mg = av(15360, 8192).rearrange("p (k n) -> p k n", k=16)
                bsl = av(23552, 7168).rearrange("p (k n) -> p k n", k=14)
                BRK = [(0, 4), (4, 2), (6, 4), (10, 4)]
                for blk in range(T // 512):
                    t0 = blk * 512
                    dma("sp", hT3, HT[:, :, t0:t0 + 512].rearrange("k p n -> p k n"), [("HT", blk // 2)], ["hT3"], "a3h")
                    dma("sp", yT3, YT[:, :, t0:t0 + 512].rearrange("k p n -> p k n"), [], ["yT3"], "a3y")
                    for jq in range(4):
                        dma("sp", bsl, WB[l][S_BR + jq][:, 0:7168].rearrange("p (k n) -> p k n", k=14), [("wb", l, S_BR + jq)], ["bsl"], "a3b")
                        for jj in range(4):
                            j = 4 * jq + jj
                            k = load_slab(l, S_G + j)
                            gv = wbuf[k][:].rearrange("p (k n) -> p k n", k=16)
                            accf, acc_r = F(12 + nxt("macc", 2))
                            for i in range(4):
                                pg, pg_r = ps(nxt("pg", 2))
                                for kc in range(16):
                                    mm(pg[:], gv[:, kc, i * 128:(i + 1) * 128], hT3[:, kc, :], kc == 0, kc == 15, [("wbuf", k), "hT3"], [pg_r])
                                pp, pp_r = ps(2 + nxt("pp", 2))
                                k0, nk = BRK[i]
                                for u in range(nk):
                                    mm(pp[:], bsl[:, k0 + u, jj * 128:(jj + 1) * 128], yT3[:, k0 + u, :], u == 0, u == nk - 1, ["bsl", "yT3"], [pp_r])
                                sg, sg_r = F(8 + nxt("sg", 2))
                                act(sg, pg[:], AF.Sigmoid, [pg_r], [sg_r])
                                if i == 0:
                                    tt("dve", accf, pp[:], sg, ALU.mult, [pp_r, sg_r], [acc_r])
                                else:
                                    tmp, tmp_r = F(10 + nxt("mtmp", 2))
                                    tt("dve", tmp, pp[:], sg, ALU.mult, [pp_r, sg_r], [tmp_r])
                                    if i < 3:
                                        tt("dve", accf, accf, tmp, ALU.add, [acc_r, tmp_r], [acc_r])
                                    else:
                                        tt("dve", mg[:, j, :], accf, tmp, ALU.add, [acc_r, tmp_r], ["mg"])
                    for s in range(4):
                        k = load_slab(l, S_WO + s)
                        wv = wbuf[k][:].rearrange("p (k n) -> p k n", k=16)
                        for jj in range(4):
                            j = 4 * s + jj
                            po2, po2_r = ps(4 + nxt("po3", 2))
                            for kc in range(16):
                                mm(po2[:], wv[:, kc, jj * 128:(jj + 1) * 128], mg[:, kc, :], kc == 0, kc == 15, [("wbuf", k), "mg"], [po2_r])
                            xc, xc_r = F(nxt("xc", 2))
                            dma("sp", xc, XT[j, :, t0:t0 + 512], [("XT", blk, j)], [xc_r], "xc%d" % (rot["xc"] % 2))
                            xn, xn_r = F(2 + nxt("xn", 2))
                            tt("dve", xn, po2[:], xc, ALU.add, [po2_r, xc_r], [xn_r])
                            dma("pool", XT[j, :, t0:t0 + 512], xn, [xn_r], [("XT", blk, j)], "xn%d" % (rot["xn"] % 2))
                S.barrier()

            if "mlp" in ph:
                h2 = av(0, 8192).rearrange("p (k n) -> p k n", k=16)
                hid = av(8192, 32768).rearrange("p (k n) -> p k n", k=64)
                for blk in range(T // 512):
                    t0 = blk * 512
                    norm_fm(t0, gmlp, "gmlp", h2, "h2")
                    for s in range(16):
                        k = load_slab(l, S_UP + s)
                        wv = wbuf[k][:].rearrange("p (k n) -> p k n", k=16)
                        for ci in range(4):
                            pu, pu_r = ps(nxt("pu", 4))
                            for kc in range(16):
                                mm(pu[:], wv[:, kc, ci * 128:(ci + 1) * 128], h2[:, kc, :], kc == 0, kc == 15, [("wbuf", k), "h2"], [pu_r])
                            rl, rl_r = F(8 + nxt("rl", 4))
                            act(rl, pu[:], AF.Relu, [pu_r], [rl_r])
                            tt("dve" if ci % 2 else "pool", hid[:, 4 * s + ci, :], rl, rl, ALU.mult, [rl_r], [("hid", 4 * s + ci)])
                    for j in range(16):
                        k = load_slab(l, S_DN + j)
                        wv = wbuf[k][:].rearrange("p (k n) -> p k n", k=64)
                        pd, pd_r = ps(4 + nxt("pd", 2))
                        for kc in range(64):
                            mm(pd[:], wv[:, kc, :], hid[:, kc, :], kc == 0, kc == 63, [("wbuf", k), ("hid", kc)], [pd_r])
                        xc, xc_r = F(12 + nxt("xc2", 2))
                        dma("sp", xc, XT[j, :, t0:t0 + 512], [("XT", blk, j)], [xc_r], "xc%d" % (rot["xc2"] % 2))
                        xn, xn_r = F(14 + nxt("xn2", 2))
                        tt("dve", xn, pd[:], xc, ALU.add, [pd_r, xc_r], [xn_r])
                        dma("pool", XT[j, :, t0:t0 + 512], xn, [xn_r], [("XT", blk, j)], "xn%d" % (rot["xn2"] % 2))
                S.barrier()

        if "fin" in ph:
            gfin = avf(0, 2048)
            dma("sp", gfin, norm_final.rearrange("(o n) -> o n", o=1).to_broadcast([128, D]), [], ["gfin"], "c0", slow=True)
            for t in range(NT):
                b2 = t % 2
                xt = avf(4096 + b2 * 4096, 2048).rearrange("p (k n) -> p k n", k=16)
                xtm = avf(12288 + b2 * 4096, 2048)
                yo = avf(20480 + b2 * 4096, 2048)
                dma("sp", xt, XT[:, :, t * 128:(t + 1) * 128].rearrange("k p n -> p k n"), [], [("fxt", b2)], "xin%d" % b2)
                for q in range(4):
                    pi = b2 * 4 + q
                    for j in range(4):
                        kc = 4 * q + j
                        S.add("pe", lambda e, o=PS[pi][:, j * 128:(j + 1) * 128], i=xt[:, kc, :]: e.transpose(o, i, identf),
                              [("fxt", b2), "cf32"], [("ps", pi)])
                    copy(evac_eng(), xtm[:, q * 512:(q + 1) * 512], PS[pi][:], [("ps", pi)], [("xtm", b2, q)])
                jk, jk_r = F(nxt("fj", 2))
                ss = small[:, b2:b2 + 1]
                for q in range(4):
                    act(jk, xtm[:, q * 512:(q + 1) * 512], AF.Square, [("xtm", b2, q)], [jk_r], accum_out=small[:, 4 + q:5 + q])
                tt("dve", small[:, 4:5], small[:, 4:5], small[:, 5:6], ALU.add, [jk_r], [jk_r])
                tt("dve", small[:, 6:7], small[:, 6:7], small[:, 7:8], ALU.add, [jk_r], [jk_r])
                tt("dve", ss, small[:, 4:5], small[:, 6:7], ALU.add, [jk_r], [("ss", b2)])
                act(ss, ss, AF.Sqrt, [("ss", b2), "epsb"], [("ss", b2)], bias=epsb[:], scale=1.0 / D)
                recip(ss, ss, [("ss", b2)], [("ss", b2)])
                stt(yo, xtm, ss, gfin, ALU.mult, ALU.mult, [("xtm", b2, q) for q in range(4)] + [("ss", b2), "gfin"], [("yo", b2)])
                dma("pool", y_out[t * 128:(t + 1) * 128, :], yo, [("yo", b2)], ["yout"], "xst%d" % b2)
        S.barrier()
        S.add("sp", lambda e: e.nop(), (), ())
        S.emit(nc, st)
    return nc, S


T_FULL = 8192
DEPTH = 4
_static = None


def kernel(x_prompt, x_sample, norm_mix, w_in, rel_bias_a, q_gain_c, k_gain_c, sink_d, w_branch_a, w_branch_b,
           w_branch_c, w_branch_d, w_out, norm_mlp, w_up, w_down, norm_final):
    f = lambda a: np.ascontiguousarray(np.asarray(a, dtype=np.float32))
    x_prompt = f(x_prompt)
    x_sample = f(x_sample)
    T = T_FULL
    stt = static_tables()
    tab_p = core_tables([8192], T)
    tab_s = core_tables([2048] * 4, T)
    shared = {"norm_mix": f(norm_mix), "w_in": f(w_in), "rel_bias_a": f(rel_bias_a), "q_gain_c": f(q_gain_c),
              "k_gain_c": f(k_gain_c), "sink_d": f(sink_d), "w_branch_a": f(w_branch_a), "w_branch_b": f(w_branch_b),
              "w_branch_c": f(w_branch_c), "w_branch_d": f(w_branch_d), "w_out": f(w_out), "norm_mlp": f(norm_mlp),
              "w_up": f(w_up), "w_down": f(w_down), "norm_final": f(norm_final)}
    shared.update(stt)
    work = {0: (x_prompt[0], tab_p), 1: (x_prompt[1], tab_p), 4: (x_sample.reshape(T, D), tab_s)}
    zeros = np.zeros((T, D), np.float32)
    in_maps = []
    for c in range(8):
        xc, tb = work.get(c, (zeros, tab_p))
        m = dict(shared)
        m["x"] = xc
        m.update(tb)
        in_maps.append(m)
    nc, _ = build_program(T, DEPTH)
    res = run_bass_kernel_spmd(nc, in_maps, core_ids=list(range(8)))
    y0 = np.asarray(res.results[0]["y"], dtype=np.float32)
    y1 = np.asarray(res.results[1]["y"], dtype=np.float32)
    y2 = np.asarray(res.results[4]["y"], dtype=np.float32)
    y_prompt = np.stack([y0, y1], axis=0)
    y_sample = y2.reshape(4, 2048, D)
    return (y_prompt, y_sample)
```

```python
import contextlib
import os
import numpy as np
import ml_dtypes
import concourse.bass as bass
import concourse.mybir as mybir
from concourse.bass_utils import run_bass_kernel_spmd

F32 = mybir.dt.float32
BF16 = mybir.dt.bfloat16
AF = mybir.ActivationFunctionType
ALU = mybir.AluOpType

D = 2048
KC = 16
DFF = 8192
AQ, AK, AV, BQ, BK, BV, CQ, CK, CV, DQ, DK, DV, GL = 0, 512, 1024, 1536, 2304, 3072, 3840, 4352, 4480, 4608, 5120, 5248, 5376
INW = 13568
NEG = -30000.0
EPS = 1e-6
FMCOLS = ([AQ + 128 * i for i in range(4)] + [AK + 128 * i for i in range(4)] + [BQ + 128 * i for i in range(6)]
          + [BK + 128 * i for i in range(6)] + [CQ + 128 * i for i in range(4)] + [CK]
          + [DQ + 128 * i for i in range(4)] + [DK])
QA0, KA0, QB0, KB0, QC0, KC0, QD0, KD0 = 0, 4, 8, 14, 20, 24, 25, 29
TMPIECES = [[(AV, 512, 0)], [(BV, 512, 0)], [(BV + 512, 256, 0), (CV, 128, 256), (DV, 128, 384)]]
YA0, YB0, YC0, YD0 = 0, 4, 6, 10
S_FM, S_TM, S_G, S_BR, S_WO, S_UP, S_DN, NSLAB = 0, 8, 11, 27, 31, 35, 51, 67


class Sched:
    ENGS = ("pe", "act", "dve", "pool", "sp")

    def __init__(self):
        self.ops = []

    def add(self, eng, fn, reads=(), writes=(), sem=None, tag=None):
        self.ops.append((eng, fn, tuple(reads), tuple(writes), sem, tag))

    def barrier(self):
        self.ops.append(None)

    def emit(self, nc, stack):
        ops = self.ops
        n = len(ops)
        chan = [None] * n
        for i, o in enumerate(ops):
            if o is not None:
                chan[i] = ("d", o[4]) if o[4] is not None else ("e", o[0])
        W = {}
        red = [None] * n
        needs = [False] * n
        lastchan = {}
        pending = {}
        for i, o in enumerate(ops):
            if o is None:
                snap = dict(lastchan)
                pending = {e: snap for e in self.ENGS}
                W = {}
                continue
            eng, fn, reads, writes, sem, tag = o
            d = set()
            for r in reads:
                st = W.get(r)
                if st is not None:
                    d.update(st[0])
            for w in writes:
                st = W.get(w)
                if st is None:
                    W[w] = [[i], tag, [], []]
                elif tag is not None and st[1] == tag:
                    d.update(st[2])
                    st[0].append(i)
                else:
                    bd = st[0] + st[3]
                    d.update(bd)
                    W[w] = [[i], tag, bd, []]
            for r in reads:
                st = W.get(r)
                if st is None:
                    W[r] = [[], None, [], [i]]
                else:
                    st[3].append(i)
            d.discard(i)
            best = {}
            pb = pending.pop(eng, None)
            if pb:
                best.update(pb)
            for j in d:
                c = chan[j]
                if c[0] == "d":
                    j = lastchan[c]
                if c[0] == "e" and c[1] == "pe" and eng == "pe":
                    continue
                if c not in best or best[c] < j:
                    best[c] = j
            if eng == "pe":
                best.pop(("e", "pe"), None)
            red[i] = best
            for j in best.values():
                needs[j] = True
            lastchan[chan[i]] = i
        sigval = [0] * n
        cnt = {}
        for i in range(n):
            c = chan[i]
            if c is None:
                continue
            if c[0] == "d":
                cnt[c] = cnt.get(c, 0) + 16
                sigval[i] = cnt[c]
            elif needs[i]:
                cnt[c] = cnt.get(c, 0) + 1
                sigval[i] = cnt[c]
        sems = {}
        for c in cnt:
            sems[c] = stack.enter_context(nc.semaphore("s_%s_%s" % (c[0], c[1])))
        per_eng = {e: [] for e in self.ENGS}
        for i, o in enumerate(ops):
            if o is not None:
                per_eng[o[0]].append(i)
        stats = {e: [0, 0] for e in self.ENGS}

        def run(ename, e):
            waited = {}
            for i in per_eng[ename]:
                for c, j in red[i].items():
                    v = sigval[j]
                    if waited.get(c, 0) < v:
                        e.wait_ge(sems[c], v)
                        waited[c] = v
                        stats[ename][1] += 1
                ins = ops[i][1](e)
                stats[ename][0] += 1
                c = chan[i]
                if c[0] == "d":
                    ins.then_inc(sems[c], 16)
                elif needs[i]:
                    ins.then_inc(sems[c], 1)

        with nc.Block() as block:
            @block.tensor
            def _(e):
                run("pe", e)

            @block.scalar
            def _(e):
                run("act", e)

            @block.vector
            def _(e):
                run("dve", e)

            @block.gpsimd
            def _(e):
                run("pool", e)

            @block.sync
            def _(e):
                run("sp", e)
        self.stats = stats
        self.nsems = len(sems)


def alibi_slopes(n):
    return (2.0 ** (-8.0 * np.arange(1, n + 1) / n)).astype(np.float32)


def static_tables():
    bf = ml_dtypes.bfloat16
    ident = np.eye(128, dtype=np.float32)
    ones = np.ones((128, 128), np.float32)
    bd = np.zeros((128, 128), np.float32)
    bd[:64, :64] = 1.0
    bd[64:, 64:] = 1.0
    perm = np.zeros((128, 128), np.float32)
    for m in range(128):
        hb, dd = m // 64 * 64, m % 64
        w = dd % 32
        partner = dd + 16 if w < 16 else dd - 16
        perm[hb + partner, m] = 1.0
    cf32 = np.concatenate([ident, ones, bd, perm], axis=1)
    kk = np.arange(128)[:, None]
    qq = np.arange(128)[None, :]
    sd = alibi_slopes(8)
    biasd = np.zeros((128, 3, 2, 4, 128), np.float32)
    for ri, rel in enumerate((-1, 0, 1)):
        o = kk + 128 * rel - qq
        for hf in range(2):
            for j in range(4):
                h = 2 * j + hf
                biasd[:, ri, hf, j, :] = np.where(np.abs(o) <= 128, -8.0 * sd[h] * np.abs(o), NEG)
    sb = alibi_slopes(12)
    biasb = np.zeros((128, 3, 3, 2, 2, 128), np.float32)
    for g, dil in enumerate((1, 4, 16)):
        for ri, rel in enumerate((-1, 0, 1)):
            o = kk + 128 * rel - qq
            for hf in range(2):
                for j in range(2):
                    s = 2 * j + hf
                    biasb[:, g, ri, hf, j, :] = np.where(np.abs(o) <= 64, -8.0 * sb[4 * g + s] * dil * np.abs(o), NEG)
    return {"cf32": cf32, "biasd": biasd.reshape(128, 2 * 3 * 512).astype(bf),
            "biasb": biasb.reshape(128, 3 * 3 * 512).astype(bf)}


def core_tables(segs, T):
    bf = ml_dtypes.bfloat16
    NT = T // 128
    NW = T // 2048
    assert sum(segs) == T
    seg_of_tok = np.zeros(T, np.int64)
    seg_start = np.zeros(T, np.int64)
    seg_len = np.zeros(T, np.int64)
    p = 0
    for si, L in enumerate(segs):
        seg_of_tok[p:p + L] = si
        seg_start[p:p + L] = p
        seg_len[p:p + L] = L
        p += L
    t_loc = np.arange(T) - seg_start
    row = (t_loc // 64).astype(np.float32)
    col = (t_loc % 64).astype(np.float32)
    freqs = (np.float32(10000.0) ** (-np.arange(16, dtype=np.float32) / np.float32(16))).astype(np.float32)
    rope = np.zeros((2, 128, T), np.float32)
    for pp in range(128):
        dd = pp % 64
        half, w = dd // 32, dd % 32
        i = w % 16
        pos = row if half == 0 else col
        ang = (pos * freqs[i]).astype(np.float32)
        rope[0, pp] = np.cos(ang)
        rope[1, pp] = np.sin(ang) * (-1.0 if w < 16 else 1.0)
    ncol = 13 * NT + NT * NW + 1
    mc = np.zeros(ncol, np.float32)
    segt = seg_of_tok[::128]
    for b in range(NT):
        for ri, rel in enumerate((-1, 0, 1)):
            k = b + rel
            ok = 0 <= k < NT and segt[k] == segt[b]
            mc[b * 3 + ri] = 0.0 if ok else NEG
    base = 3 * NT
    for g, dil in enumerate((1, 4, 16)):
        nmi = NT // dil
        for r in range(dil):
            for mi in range(nmi):
                for ri, rel in enumerate((-1, 0, 1)):
                    k = mi + rel
                    ok = 0 <= k < nmi and seg_of_tok[dil * 128 * k] == seg_of_tok[dil * 128 * mi]
                    mc[base + g * 3 * NT + (r * nmi + mi) * 3 + ri] = 0.0 if ok else NEG
    basec = 12 * NT
    for b in range(NT):
        for kw in range(NW):
            ok = seg_of_tok[2048 * kw] == segt[b]
            mc[basec + b * NW + kw] = 0.0 if ok else NEG
    mcols = np.broadcast_to(mc[None, :], (128, ncol)).copy()
    ma = np.zeros((NT, 128, 7, 128), np.float32)
    qrl = np.arange(128) // 64
    qc = np.arange(128) % 64
    c0 = np.clip(qc - 8, 0, 48)
    for b in range(NT):
        rs = seg_start[b * 128] // 64
        R = seg_len[b * 128] // 64
        qr = 2 * b + qrl
        r0 = np.clip(qr - rs - 4, 0, R - 8) + rs
        for sl in range(7):
            kt = b + sl - 3
            if kt < 0 or kt >= NT:
                continue
            kr = 2 * kt + qrl
            kcol = qc
            rowok = (kr[:, None] >= r0[None, :]) & (kr[:, None] < r0[None, :] + 8)
            colok = (kcol[:, None] >= c0[None, :]) & (kcol[:, None] < c0[None, :] + 16)
            ma[b, :, sl, :] = (rowok & colok).astype(np.float32)
    return {"rope": rope, "mcols": mcols, "maska": ma.astype(bf)}


def build_program(T, depth, phases=None, debug=False):
    assert T % 2048 == 0
    NT = T // 128
    NW = T // 2048
    NCOL = 13 * NT + NT * NW + 1
    ZCOL = NCOL - 1
    ph = phases or {"pre", "p0", "p1", "A", "B", "C", "D", "p3", "mlp", "fin"}
    nc = bass.Bass("TRN2", target_bir_lowering=False)
    S = Sched()
    st = contextlib.ExitStack()
    with st:
        def din(name, shape, dt=F32):
            return nc.dram_tensor(name, list(shape), dt, kind="ExternalInput")

        x_in = din("x", [T, D]).ap()
        norm_mix = din("norm_mix", [depth, D]).ap()
        w_in = din("w_in", [depth, D, INW]).ap()
        rel_bias_a = din("rel_bias_a", [depth, 8, 15, 31]).ap()
        q_gain = din("q_gain_c", [depth, 64]).ap()
        k_gain = din("k_gain_c", [depth, 64]).ap()
        sink_d = din("sink_d", [depth, 8]).ap()
        w_ba = din("w_branch_a", [depth, 512, D]).ap()
        w_bb = din("w_branch_b", [depth, 256, D]).ap()
        w_bc = din("w_branch_c", [depth, 512, D]).ap()
        w_bd = din("w_branch_d", [depth, 512, D]).ap()
        w_out = din("w_out", [depth, D, D]).ap()
        norm_mlp = din("norm_mlp", [depth, D]).ap()
        w_up = din("w_up", [depth, D, DFF]).ap()
        w_down = din("w_down", [depth, DFF, D]).ap()
        norm_final = din("norm_final", [D]).ap()
        rope_in = din("rope", [2, 128, T]).ap()
        mcols_in = din("mcols", [128, NCOL]).ap()
        maska_in = din("maska", [NT, 128, 7, 128], BF16).ap()
        cf32_in = din("cf32", [128, 512]).ap()
        biasd_in = din("biasd", [128, 3072], BF16).ap()
        biasb_in = din("biasb", [128, 4608], BF16).ap()
        y_out = nc.dram_tensor("y", [T, D], F32, kind="ExternalOutput").ap()
        okind = "ExternalOutput" if debug else "Internal"
        XT = nc.dram_tensor("XT", [KC, 128, T], F32, kind=okind).ap()
        HT = nc.dram_tensor("HT", [KC, 128, T], BF16).ap()
        QK = nc.dram_tensor("QK", [30, 128, T], BF16, kind=okind).ap()
        VTM_h = nc.dram_tensor("VTM", [T, 1536], BF16, kind=okind)
        VTM = VTM_h.ap()
        YT = nc.dram_tensor("YT", [14, 128, T], BF16, kind=okind).ap()
        WB = [nc.dram_tensor("WB%d" % l, [NSLAB, 128, 8192], BF16).ap() for l in range(depth)]
        EXT = nc.dram_tensor("EXT", [depth, 8, 15, 127], F32)

        def sbt(name, shape, dt):
            return st.enter_context(nc.sbuf_tensor("sb_" + name, list(shape), dt))

        ARN = 43008
        arena = sbt("arena", [128, ARN], BF16)
        wbuf = [sbt("wbuf%d" % i, [128, 8192], BF16) for i in range(3)]
        NF = 16
        Ft = [sbt("F%d" % i, [128, 512], F32) for i in range(NF)]
        NB = 8
        Bt = [sbt("B%d" % i, [128, 512], BF16) for i in range(NB)]
        cf32 = sbt("cf32", [128, 512], F32)
        identb = sbt("identb", [128, 128], BF16)
        biasd = sbt("biasd", [128, 3072], BF16)
        biasb = sbt("biasb", [128, 4608], BF16)
        mcols = sbt("mcols", [128, NCOL], F32)
        esb = sbt("esb", [64, 2, 512], F32)
        gmix = sbt("gmix", [128, 16], F32)
        gmlp = sbt("gmlp", [128, 16], F32)
        gains = sbt("gains", [128, 2], F32)
        epsb = sbt("epsb", [128, 1], F32)
        es8 = sbt("es8", [64, 8], F32)
        small = sbt("small", [128, 8], F32)
        PS = [st.enter_context(nc.psum_tensor("ps%d" % i, [128, 512], F32)) for i in range(8)]
        identf = cf32[:, 0:128]
        onesf = cf32[:, 128:256]
        bdf = cf32[:, 256:384]
        permf = cf32[:, 384:512]

        def av(off, n):
            return arena[:, off:off + n]

        def avf(off, n):
            return arena[:, off:off + 2 * n].bitcast(F32)

        SEMMAP = {"xin0": "g0", "xin1": "g1", "xs0": "g0", "xs1": "g1", "rp0": "g2", "rp1": "g3", "xc0": "g4", "xc1": "g5",
                  "aq": "g0", "aq0": "g1", "aq1": "g2", "ak": "g3", "avv": "g4", "avv0": "g4", "avv1": "g5", "am": "g6",
                  "a3h": "g0", "a3y": "g1", "a3b": "g2", "a3c": "g3", "c0": "g7",
                  "xst0": "q0", "xst1": "q1", "hts": "q0", "qks0": "q1", "qks1": "q2", "vst0": "q3", "vst1": "q4", "vst2": "q5",
                  "vst3": "q6", "yst0": "q0", "yst1": "q1", "yst2": "q2", "yst3": "q3", "xn0": "q0", "xn1": "q1", "c1": "q7"}

        def dma(q, out, in_, reads, writes, sem, tag=None, slow=False):
            if sem in SEMMAP:
                sem = SEMMAP[sem]
            assert sem[0] in ("w", "cv"[0], "g", "q"), sem
            assert (q == "pool") == (sem[0] in ("q", "c")), (q, sem)
            if slow:
                S.add(q, lambda e, o=out, i=in_: e.dma_start(out=o, in_=i, allow_slow_non_contiguous=True),
                      reads, writes, sem=sem, tag=tag)
            else:
                S.add(q, lambda e, o=out, i=in_: e.dma_start(out=o, in_=i), reads, writes, sem=sem, tag=tag)

        def mm(out, lhsT, rhs, start, stop, reads, writes):
            S.add("pe", lambda e, o=out, l=lhsT, r=rhs, a=start, b=stop:
                  e.matmul(o, lhsT=l, rhs=r, start=a, stop=b, skip_group_check=True), reads, writes)

        def act(out, in_, func, reads, writes, bias=None, scale=None, accum_out=None):
            kw = {}
            if bias is not None:
                kw["bias"] = bias
            if scale is not None:
                kw["scale"] = scale
            if accum_out is not None:
                kw["accum_out"] = accum_out
            S.add("act", lambda e, o=out, i=in_, f=func, k=kw: e.activation(out=o, in_=i, func=f, **k), reads, writes)

        def tt(eng, out, in0, in1, op, reads, writes):
            S.add(eng, lambda e, o=out, a=in0, b=in1, p=op: e.tensor_tensor(out=o, in0=a, in1=b, op=p), reads, writes)

        def stt(out, in0, scalar, in1, op0, op1, reads, writes):
            S.add("dve", lambda e, o=out, a=in0, s=scalar, b=in1, p0=op0, p1=op1:
                  e.scalar_tensor_tensor(out=o, in0=a, scalar=s, in1=b, op0=p0, op1=p1), reads, writes)

        def tsc(eng, out, in0, s1, op0, reads, writes, s2=None, op1=None):
            if op1 is None:
                S.add(eng, lambda e, o=out, a=in0, s=s1, p=op0: e.tensor_scalar(out=o, in0=a, scalar1=s, scalar2=None, op0=p),
                      reads, writes)
            else:
                S.add(eng, lambda e, o=out, a=in0, s=s1, t=s2, p=op0, q=op1:
                      e.tensor_scalar(out=o, in0=a, scalar1=s, scalar2=t, op0=p, op1=q), reads, writes)

        def copy(eng, out, in_, reads, writes):
            if eng == "act":
                act(out, in_, AF.Copy, reads, writes)
            else:
                S.add(eng, lambda e, o=out, i=in_: e.tensor_copy(out=o, in_=i), reads, writes)

        def recip(out, in_, reads, writes):
            S.add("dve", lambda e, o=out, i=in_: e.reciprocal(out=o, in_=i), reads, writes)

        def memset(eng, ap, val, writes):
            S.add(eng, lambda e, a=ap, v=val: e.memset(a, v), (), writes)

        class Pipe:
            def __init__(self):
                self.pend = None

            def unit(self, s1, s2):
                s1()
                if self.pend is not None:
                    self.pend()
                self.pend = s2

            def flush(self):
                if self.pend is not None:
                    self.pend()
                    self.pend = None

        rot = {}

        def nxt(key, n):
            v = rot.get(key, 0)
            rot[key] = v + 1
            return v % n

        def F(i):
            return Ft[i][:], ("F", i)

        def B(i):
            return Bt[i][:], ("B", i)

        def ps(i):
            return PS[i], ("ps", i)

        alt = [0]

        def evac_eng():
            alt[0] += 1
            return "act" if alt[0] % 2 else "dve"

        dma("sp", cf32[:], cf32_in, [], ["cf32"], "c0")
        dma("sp", biasd[:], biasd_in, [], ["biasd"], "c0")
        dma("sp", biasb[:], biasb_in, [], ["biasb"], "c0")
        dma("sp", mcols[:], mcols_in, [], ["mcols"], "c0")
        copy("dve", identb[:], identf, ["cf32"], ["identb"])
        memset("dve", epsb[:], EPS, ["epsb"])
        memset("pool", arena[:], 0.0, ["arena0"])
        for i in range(NF):
            memset("dve", Ft[i][:], 0.0, [("F", i)])
        if "A" in ph:
            for l in range(depth):
                for h in range(8):
                    src = Ft[0][0:15, 0:127]
                    dma("pool", bass.AP(tensor=EXT, offset=(l * 8 + h) * 15 * 127, ap=[[127, 15], [1, 127]]), src,
                        [("F", 0)], [("ext", l)], "c1", tag=("extz", l))
        S.barrier()

        cvn = [0]

        def wsrc(w2d, c0, ncols):
            return w2d[:, c0:c0 + ncols].rearrange("(k p) n -> p k n", p=128)

        def convert(l, slab, pieces):
            k = cvn[0] % 4
            cvn[0] += 1
            tag = ("cv", cvn[0])
            for dst, src in pieces:
                dma("pool", dst, src, [], [("wb", l, slab), ("cvslot", k)], "cv%d" % k, tag=tag)

        def slabv(l, s, kk, nn):
            return WB[l][s][:, 0:kk * nn].rearrange("p (k n) -> p k n", k=kk)

        if "pre" in ph:
            for l in range(depth):
                for s in range(8):
                    chunks = list(range(4 * s, min(4 * s + 4, 30)))
                    pieces = []
                    i = 0
                    while i < len(chunks):
                        j = i
                        while j + 1 < len(chunks) and FMCOLS[chunks[j + 1]] == FMCOLS[chunks[j]] + 128:
                            j += 1
                        n = j - i + 1
                        pieces.append((slabv(l, S_FM + s, 16, 512)[:, :, i * 128:(i + n) * 128],
                                       wsrc(w_in[l], FMCOLS[chunks[i]], n * 128)))
                        i = j + 1
                    if s == 7:
                        pieces.append((slabv(l, S_FM + s, 16, 512)[:, :, 256:384], wsrc(w_in[l], FMCOLS[28], 128)))
                        pieces.append((slabv(l, S_FM + s, 16, 512)[:, :, 384:512], wsrc(w_in[l], FMCOLS[29], 128)))
                    convert(l, S_FM + s, pieces)
                for s in range(3):
                    convert(l, S_TM + s, [(slabv(l, S_TM + s, 16, 512)[:, :, d0:d0 + n], wsrc(w_in[l], c0, n))
                                          for (c0, n, d0) in TMPIECES[s]])
                for j in range(16):
                    convert(l, S_G + j, [(slabv(l, S_G + j, 16, 512)[:, :, i * 128:(i + 1) * 128],
                                          wsrc(w_in[l], GL + i * 2048 + j * 128, 128)) for i in range(4)])
                for jq in range(4):
                    bv = slabv(l, S_BR + jq, 14, 512)
                    convert(l, S_BR + jq, [(bv[:, 0:4, :], wsrc(w_ba[l], jq * 512, 512)),
                                           (bv[:, 4:6, :], wsrc(w_bb[l], jq * 512, 512)),
                                           (bv[:, 6:10, :], wsrc(w_bc[l], jq * 512, 512)),
                                           (bv[:, 10:14, :], wsrc(w_bd[l], jq * 512, 512))])
                for s in range(4):
                    convert(l, S_WO + s, [(slabv(l, S_WO + s, 16, 512), wsrc(w_out[l], s * 512, 512))])
                for s in range(16):
                    convert(l, S_UP + s, [(slabv(l, S_UP + s, 16, 512), wsrc(w_up[l], s * 512, 512))])
                for j in range(16):
                    convert(l, S_DN + j, [(slabv(l, S_DN + j, 64, 128), wsrc(w_down[l], j * 128, 128))])

        def load_slab(l, slab, nel=8192):
            k = nxt("w", 3)
            dma("sp", wbuf[k][:, 0:nel], WB[l][slab][:, 0:nel], [("wb", l, slab)], [("wbuf", k)], "w%d" % k)
            return k

        if "p0" in ph:
            for t in range(NT):
                b2 = t % 2
                xin = avf(b2 * 4096, 2048)
                xst = avf(8192 + b2 * 4096, 2048).rearrange("p (k n) -> p k n", k=16)
                dma("sp", xin, x_in[t * 128:(t + 1) * 128, :], [], [("xin", b2)], "xin%d" % b2)
                for q in range(4):
                    pi = (t % 2) * 4 + q
                    for j in range(4):
                        kc = 4 * q + j
                        S.add("pe", lambda e, o=PS[pi][:, j * 128:(j + 1) * 128], i=xin[:, kc * 128:(kc + 1) * 128]:
                              e.transpose(o, i, identf), [("xin", b2), "cf32"], [("ps", pi)])
                    copy(evac_eng(), xst[:, 4 * q:4 * q + 4, :], PS[pi][:].rearrange("p (k n) -> p k n", k=4),
                         [("ps", pi)], [("xst", b2, q)])
                dma("pool", XT[:, :, t * 128:(t + 1) * 128].rearrange("k p n -> p k n"), xst,
                    [("xst", b2, q) for q in range(4)], [("XT", t // 4)], "xst%d" % b2)
            S.barrier()

        def norm_fm(tok0, gt, gres, hview, hres):
            rstd, rstd_r = F(4)
            sd, sd_r = F(5)
            psn, psn_r = ps(7)
            for pas in range(2):
                for grp in range(8):
                    xb = nxt("xs", 2)
                    xs = [Ft[2 * xb][:], Ft[2 * xb + 1][:]]
                    for u in range(2):
                        kc = 2 * grp + u
                        dma("sp", xs[u], XT[kc, :, tok0:tok0 + 512], [("XT", tok0 // 512, kc)], [("xs", xb)],
                            "xs%d" % xb, tag=("xs", rot["xs"]))
                    for u in range(2):
                        kc = 2 * grp + u
                        if pas == 0:
                            sq, sq_r = F(6 + nxt("sq", 2))
                            tt("pool", sq, xs[u], xs[u], ALU.mult, [("xs", xb)], [sq_r])
                            mm(psn[:], onesf, sq, kc == 0, kc == 15, [sq_r, "cf32"], [psn_r])
                        else:
                            stt(hview[:, kc, :], xs[u], gt[:, kc:kc + 1], rstd, ALU.mult, ALU.mult,
                                [("xs", xb), gres, rstd_r], [hres])
                if pas == 0:
                    act(sd, psn[:], AF.Sqrt, [psn_r, "epsb"], [sd_r], bias=epsb[:], scale=1.0 / D)
                    recip(rstd, sd, [sd_r], [rstd_r])

        def load_gvec(dst, src_row, res):
            dma("sp", dst[:], src_row.rearrange("(k p) -> p k", p=128), [], [res], "c0", slow=True)

        for l in range(depth):
            load_gvec(gmix, norm_mix[l], "gmix")
            load_gvec(gmlp, norm_mlp[l], "gmlp")
            for hh in range(2):
                dma("sp", gains[hh * 64:(hh + 1) * 64, 0:1], q_gain[l].rearrange("(d o) -> d o", o=1), [], ["gains"], "c0",
                    tag=("gn", l), slow=True)
                dma("sp", gains[hh * 64:(hh + 1) * 64, 1:2], k_gain[l].rearrange("(d o) -> d o", o=1), [], ["gains"], "c0",
                    tag=("gn", l), slow=True)

            if "p1" in ph:
                hT = av(0, 16384).rearrange("p (k n) -> p k n", k=16)
                for blk in range(T // 1024):
                    t0 = blk * 1024
                    for sb in range(2):
                        norm_fm(t0 + sb * 512, gmix, "gmix", hT[:, :, sb * 512:(sb + 1) * 512], ("hT", sb))
                    dma("pool", HT[:, :, t0:t0 + 1024].rearrange("k p n -> p k n"), hT, [("hT", 0), ("hT", 1)],
                        [("HT", blk)], "hts")
                    for sb in range(2):
                        dma("sp", Ft[8 + sb][:], rope_in[0, :, t0 + sb * 512:t0 + (sb + 1) * 512], [], [("rope", sb)], "rp%d" % sb,
                            tag=("rp", blk))
                        dma("sp", Ft[10 + sb][:], rope_in[1, :, t0 + sb * 512:t0 + (sb + 1) * 512], [], [("rope", sb)], "rp%d" % sb,
                            tag=("rp", blk))
                    for s in range(8):
                        k = load_slab(l, S_FM + s, 8192 if s < 7 else 8192)
                        wv = wbuf[k][:].rearrange("p (k n) -> p k n", k=16)
                        for ci in range(4):
                            ch = 4 * s + ci
                            if ch >= 30:
                                break
                            sti = nxt("qkst", 2)
                            stg = [Bt[2 * sti][:], Bt[2 * sti + 1][:]]
                            for sb in range(2):
                                pi = nxt("p1ps", 4)
                                pp, pr = ps(pi)
                                for kc in range(16):
                                    mm(pp[:], wv[:, kc, ci * 128:(ci + 1) * 128], hT[:, kc, sb * 512:(sb + 1) * 512],
                                       kc == 0, kc == 15, [("wbuf", k), ("hT", sb)], [pr])
                                sres = ("B", 2 * sti + sb)
                                if QC0 <= ch <= KC0 and not os.environ.get('P1_NOC'):
                                    gcol = 0 if ch < KC0 else 1
                                    qf, qf_r = F(12)
                                    sq, sq_r = F(13)
                                    qn, qn_r = F(14)
                                    sdc, sdc_r = F(15)
                                    pb, pb_r = ps(4 + nxt("cps", 2))
                                    copy("act", qf, pp[:], [pr], [qf_r])
                                    tt("pool", sq, qf, qf, ALU.mult, [qf_r], [sq_r])
                                    mm(pb[:], bdf, sq, True, True, [sq_r, "cf32"], [pb_r])
                                    act(sdc, pb[:], AF.Sqrt, [pb_r, "epsb"], [sdc_r], bias=epsb[:], scale=1.0 / 64)
                                    recip(sdc, sdc, [sdc_r], [sdc_r])
                                    stt(qn, qf, gains[:, gcol:gcol + 1], sdc, ALU.mult, ALU.mult, [qf_r, "gains", sdc_r], [qn_r])
                                    if os.environ.get('C_LVL') == '1':
                                        copy("dve", stg[sb], qn, [qn_r], [sres])
                                        continue
                                    pb2, pb2_r = ps(4 + nxt("cps", 2))
                                    mm(pb2[:], permf, qn, True, True, [qn_r, "cf32"], [pb2_r])
                                    tt("dve", sq, pb2[:], Ft[10 + sb][:], ALU.mult, [pb2_r, ("rope", sb)], [sq_r])
                                    if os.environ.get('C_LVL') == '2':
                                        tt("dve", stg[sb], qn, sq, ALU.add, [qn_r, sq_r], [sres])
                                        continue
                                    tt("dve", qf, qn, Ft[8 + sb][:], ALU.mult, [qn_r, ("rope", sb)], [qf_r])
                                    tt("dve", stg[sb], qf, sq, ALU.add, [qf_r, sq_r], [sres])
                                else:
                                    copy(evac_eng(), stg[sb], pp[:], [pr], [sres])
                            for sb in range(2):
                                dma("pool", QK[ch, :, t0 + sb * 512:t0 + (sb + 1) * 512], stg[sb], [("B", 2 * sti + sb)],
                                    [("QK", ch, blk)], "qks%d" % sti, tag=("qks", rot["qkst"]))
                    for s in range(0 if os.environ.get('P1_NOTM') else 3):
                        k = load_slab(l, S_TM + s)
                        wv = wbuf[k][:].rearrange("p (k n) -> p k n", k=16)
                        for tti in range(8):
                            pi = nxt("p1ps", 4)
                            pp, pr = ps(pi)
                            for kc in range(16):
                                mm(pp[:], hT[:, kc, tti * 128:(tti + 1) * 128], wv[:, kc, :], kc == 0, kc == 15,
                                   [("wbuf", k), ("hT", tti // 4)], [pr])
                            vi = 4 + nxt("vst", 4)
                            copy(evac_eng(), Bt[vi][:], pp[:], [pr], [("B", vi)])
                            dma("pool", VTM[t0 + tti * 128:t0 + (tti + 1) * 128, s * 512:(s + 1) * 512], Bt[vi][:], [("B", vi)],
                                [("VTM", blk)], "vst%d" % (vi - 4))
                S.barrier()

            def attn_norm_store(po, po_r, ych0, tok0, ntok, sink_g=None):
                dn, dn_r = F(nxt("dn", 2))
                yi = nxt("yst", 4)
                yst = Bt[yi][0:64, :]
                if sink_g is not None:
                    copy("dve", dn[0:64, :], po[64:128, :], [po_r], [dn_r])
                    tt("dve", dn[0:64, :], dn[0:64, :], esb[:, sink_g, :], ALU.add, [dn_r, "esb"], [dn_r])
                    recip(dn[0:64, :], dn[0:64, :], [dn_r], [dn_r])
                else:
                    recip(dn[0:64, :], po[64:128, :], [po_r], [dn_r])
                tt("dve", yst, po[0:64, :], dn[0:64, :], ALU.mult, [po_r, dn_r], [("B", yi)])
                for half in range(2):
                    dma("pool", YT[ych0:ych0 + 2, half * 64:(half + 1) * 64, tok0:tok0 + 128].rearrange("c p n -> p c n"),
                        yst[:, half * 256:(half + 1) * 256].rearrange("p (c n) -> p c n", c=2), [("B", yi)],
                        [("YT", ych0, tok0 // 128)], "yst%d" % yi, tag=("yst", rot["yst"]))

            if "D" in ph:
                dma("sp", es8[:], sink_d[l:l + 1, :].to_broadcast([64, 8]), [], ["es8"], "c0", slow=True)
                act(es8[:], es8[:], AF.Exp, ["es8"], ["es8"])
                for g in range(2):
                    for j, h in enumerate((4 * g, 4 * g + 2, 4 * g + 1, 4 * g + 3)):
                        copy("dve", esb[:, g, j * 128:(j + 1) * 128], es8[:, h:h + 1].to_broadcast([64, 128]), ["es8"], ["esb"])
                qD = av(0, 4096).rearrange("p (c n) -> p c n", c=4)
                kD = av(4096, 2560).rearrange("p (g n) -> p g n", g=2)
                vD = av(6656, 2560).rearrange("p (t g n) -> p t g n", t=10, g=2)
                memset("dve", vD[:, :, :, 64:128], 1.0, ["vD1"])
                for sbk in range(T // 1024):
                    t0 = sbk * 1024
                    lo = max(t0 - 128, 0)
                    hi = min(t0 + 1152, T)
                    off = lo - (t0 - 128)
                    dma("sp", qD, QK[QD0:QD0 + 4, :, t0:t0 + 1024].rearrange("c p n -> p c n"), [("QK", QD0 + c, sbk) for c in range(4)],
                        ["qD"], "aq")
                    if off > 0:
                        memset("dve", kD[:, :, 0:off], 0.0, ["kD"])
                        memset("dve", vD[:, 0:off // 128, :, 0:64], 0.0, ["vD"])
                    if off + hi - lo < 1280:
                        memset("dve", kD[:, :, off + hi - lo:1280], 0.0, ["kD"])
                        memset("dve", vD[:, (off + hi - lo) // 128:10, :, 0:64], 0.0, ["vD"])
                    for g in range(2):
                        for half in range(2):
                            dma("sp", kD[half * 64:(half + 1) * 64, g, off:off + hi - lo], QK[KD0, g * 64:(g + 1) * 64, lo:hi],
                                [("QK", KD0, b) for b in range(T // 1024)], ["kD"], "ak", tag=("kD", sbk))
                        dma("sp", vD[:, off // 128:(off + hi - lo) // 128, g, 0:64],
                            VTM[lo:hi, 1408 + g * 64:1408 + (g + 1) * 64].rearrange("(t p) d -> p t d", p=128),
                            [("VTM", b) for b in range(T // 1024)], ["vD"], "avv", tag=("vD", sbk))
                    pipe = Pipe()
                    for bl in range(8):
                        b = sbk * 8 + bl
                        pob = nxt("po", 2)
                        pos = [ps(4 + 2 * pob + g) for g in range(2)]
                        for ri in range(3):
                            kt = bl + ri
                            sb2 = 2 * nxt("pss", 2)
                            pb2 = 2 * nxt("pb", 2)

                            def s1(bl=bl, b=b, ri=ri, kt=kt, sb2=sb2, pb2=pb2):
                                for hf in range(2):
                                    pss, pss_r = ps(sb2 + hf)
                                    mm(pss[:], identb[:], biasd[:, (ri * 2 + hf) * 512:(ri * 2 + hf + 1) * 512], True, False,
                                       ["identb", "biasd"], [pss_r])
                                    for g in range(2):
                                        mm(pss[:, g * 256:(g + 1) * 256].rearrange("p (c n) -> p c n", c=2),
                                           kD[hf * 64:(hf + 1) * 64, g, kt * 128:(kt + 1) * 128],
                                           qD[hf * 64:(hf + 1) * 64, 2 * g:2 * g + 2, bl * 128:(bl + 1) * 128], False, g == 1,
                                           ["kD", "qD"], [pss_r])
                                    act(Bt[4 + pb2 + hf][:], pss[:], AF.Exp, [pss_r, "mcols"], [("B", 4 + pb2 + hf)],
                                        bias=mcols[:, b * 3 + ri:b * 3 + ri + 1], scale=0.125)

                            def s2(b=b, ri=ri, kt=kt, pb2=pb2, pos=pos):
                                for g in range(2):
                                    po, po_r = pos[g]
                                    for hf in range(2):
                                        mm(po[:, hf * 256:(hf + 1) * 256], vD[:, kt, g, :], Bt[4 + pb2 + hf][:, g * 256:(g + 1) * 256],
                                           ri == 0 and hf == 0, ri == 2 and hf == 1, ["vD", "vD1", ("B", 4 + pb2 + hf)], [po_r])
                                if ri == 2:
                                    for g in range(2):
                                        attn_norm_store(pos[g][0], pos[g][1], YD0 + 2 * g, b * 128, 128, sink_g=g)
                            pipe.unit(s1, s2)
                    pipe.flush()
                S.barrier()

            if "C" in ph:
                kCt = av(0, 2 * T).rearrange("p (g n) -> p g n", g=2)
                vC = av(2 * T, NT * 256).rearrange("p (t g n) -> p t g n", t=NT, g=2)
                memset("dve", vC[:, :, :, 64:128], 1.0, ["vC1"])
                for g in range(2):
                    for half in range(2):
                        dma("sp", kCt[half * 64:(half + 1) * 64, g, :], QK[KC0, g * 64:(g + 1) * 64, :], [("QK", KC0, b) for b in range(T // 1024)],
                            ["kC"], "ak", tag="kC")
                    for tq in range(NT // 16):
                        dma("sp", vC[:, tq * 16:(tq + 1) * 16, g, 0:64],
                            VTM[tq * 2048:(tq + 1) * 2048, 1280 + g * 64:1280 + (g + 1) * 64].rearrange("(t p) d -> p t d", p=128),
                            [("VTM", b) for b in range(T // 1024)], ["vC"], "avv", tag="vC")
                pipe = Pipe()
                for b in range(NT):
                    qi = nxt("qC", 2)
                    qCt = av(2 * T + NT * 256 + qi * 512, 512).rearrange("p (c n) -> p c n", c=4)
                    dma("sp", qCt, QK[QC0:QC0 + 4, :, b * 128:(b + 1) * 128].rearrange("c p n -> p c n"),
                        [("QK", QC0 + c, b // 8) for c in range(4)], [("qC", qi)], "aq%d" % qi)
                    pob = nxt("po", 2)
                    pos = [ps(4 + 2 * pob + g) for g in range(2)]
                    for kt in range(NT):
                        sb2 = 2 * nxt("pss", 2)
                        pb2 = 2 * nxt("pb", 2)
                        mcol = 12 * NT + b * NW + kt // 16

                        def s1(kt=kt, sb2=sb2, pb2=pb2, mcol=mcol, qCt=qCt, qi=qi):
                            for hf in range(2):
                                pss, pss_r = ps(sb2 + hf)
                                for g in range(2):
                                    mm(pss[:, g * 256:(g + 1) * 256].rearrange("p (c n) -> p c n", c=2),
                                       kCt[hf * 64:(hf + 1) * 64, g, kt * 128:(kt + 1) * 128],
                                       qCt[hf * 64:(hf + 1) * 64, 2 * g:2 * g + 2, :], g == 0, g == 1, ["kC", ("qC", qi)], [pss_r])
                                act(Bt[4 + pb2 + hf][:], pss[:], AF.Exp, [pss_r, "mcols"], [("B", 4 + pb2 + hf)],
                                    bias=mcols[:, mcol:mcol + 1], scale=0.125)

                        def s2(b=b, kt=kt, pb2=pb2, pos=pos):
                            for g in range(2):
                                po, po_r = pos[g]
                                for hf in range(2):
                                    mm(po[:, hf * 256:(hf + 1) * 256], vC[:, kt, g, :], Bt[4 + pb2 + hf][:, g * 256:(g + 1) * 256],
                                       kt == 0 and hf == 0, kt == NT - 1 and hf == 1, ["vC", "vC1", ("B", 4 + pb2 + hf)], [po_r])
                            if kt == NT - 1:
                                for g in range(2):
                                    attn_norm_store(pos[g][0], pos[g][1], YC0 + 2 * g, b * 128, 128)
                        pipe.unit(s1, s2)
                pipe.flush()
                S.barrier()

            if "A" in ph:
                rbt, rbt_r = F(12)
                dma("sp", rbt[0:8, 0:465], rel_bias_a[l].rearrange("h r c -> h (r c)"), [], [rbt_r], "c0")
                tsc("dve", rbt[0:8, 0:465], rbt[0:8, 0:465], 8.0, ALU.mult, [rbt_r], [rbt_r])
                dma("pool", bass.AP(tensor=EXT, offset=l * 8 * 15 * 127 + 48, ap=[[15 * 127, 8], [127, 15], [1, 31]]),
                    rbt[0:8, 0:465].rearrange("h (r c) -> h r c", r=15), [rbt_r, ("ext", l)], [("ext2", l)], "c1")
                TOE = av(0, 7680)[0:64, :].rearrange("p (m n) -> p m n", m=120)
                dma("pool", TOE, bass.AP(tensor=EXT, offset=l * 8 * 15 * 127, ap=[[1, 64], [127, 120], [1, 64]]),
                    [("ext2", l)], ["TOE"], "c1")
                RBA = av(32768, 7168).rearrange("p (s f c n) -> p s f c n", s=7, f=2, c=4)
                memset("dve", av(32768, 7168), 0.0, ["RBA"])
                TOE4 = av(0, 7680)[0:64, :].rearrange("p (h r n) -> p h r n", h=8, r=15)
                for sl in range(7):
                    for krl in range(2):
                        for qrl in range(2):
                            dri = 2 * (sl - 3) + krl - qrl + 7
                            if 0 <= dri <= 14:
                                for hf in range(2):
                                    copy("dve", RBA[krl * 64:(krl + 1) * 64, sl, hf, :, qrl * 64:(qrl + 1) * 64],
                                         TOE4[:, hf::2, dri, ::-1], ["TOE"], ["RBA"])
                S.barrier()
                qA = av(0, 4096).rearrange("p (c n) -> p c n", c=4)
                kA = av(4096, 7168).rearrange("p (c n) -> p c n", c=4)
                vA = av(11264, 14336).rearrange("p (t h n) -> p t h n", t=14, h=8)
                mA = av(25600, 7168).rearrange("p (b s n) -> p b s n", b=8, s=7)
                memset("dve", vA[:, :, :, 64:128], 1.0, ["vA1"])
                for sbk in range(T // 1024):
                    t0 = sbk * 1024
                    lo = max(t0 - 384, 0)
                    hi = min(t0 + 1024 + 384, T)
                    off = lo - (t0 - 384)
                    dma("sp", qA, QK[QA0:QA0 + 4, :, t0:t0 + 1024].rearrange("c p n -> p c n"), [("QK", QA0 + c, sbk) for c in range(4)],
                        ["qA"], "aq")
                    if off > 0:
                        memset("dve", kA[:, :, 0:off], 0.0, ["kA"])
                        memset("dve", vA[:, 0:off // 128, :, 0:64], 0.0, ["vA"])
                    if off + hi - lo < 1792:
                        memset("dve", kA[:, :, off + hi - lo:1792], 0.0, ["kA"])
                        memset("dve", vA[:, (off + hi - lo) // 128:14, :, 0:64], 0.0, ["vA"])
                    dma("sp", kA[:, :, off:off + hi - lo], QK[KA0:KA0 + 4, :, lo:hi].rearrange("c p n -> p c n"),
                        [("QK", KA0 + c, b) for c in range(4) for b in range(T // 1024)], ["kA"], "ak")
                    for tq in range(off // 128, (off + hi - lo) // 128):
                        tk = (t0 - 384) // 128 + tq
                        dma("sp", vA[:, tq, :, 0:64], VTM[tk * 128:(tk + 1) * 128, 0:512].rearrange("p (h d) -> p h d", h=8),
                            [("VTM", b) for b in range(T // 1024)], ["vA"], "avv", tag=("vA", sbk))
                    dma("sp", mA, maska_in[sbk * 8:(sbk + 1) * 8].rearrange("b p s n -> p b s n"), [], ["mA"], "am")
                    pipe = Pipe()
                    for bl in range(8):
                        b = sbk * 8 + bl
                        pob = nxt("po", 2)
                        pos = [ps(4 + 2 * pob + hf) for hf in range(2)]
                        for sl in range(7):
                            kt = bl + sl
                            sb2 = 2 * nxt("pss", 2)
                            pb2 = 2 * nxt("pb", 2)

                            def s1(bl=bl, sl=sl, kt=kt, sb2=sb2, pb2=pb2):
                                for hf in range(2):
                                    pss, pss_r = ps(sb2 + hf)
                                    mm(pss[:].rearrange("p (c n) -> p c n", c=4), identb[:], RBA[:, sl, hf, :, :], True, False, ["identb", "RBA"], [pss_r])
                                    for c in range(4):
                                        mm(pss[:, c * 128:(c + 1) * 128], kA[hf * 64:(hf + 1) * 64, c, kt * 128:(kt + 1) * 128],
                                           qA[hf * 64:(hf + 1) * 64, c, bl * 128:(bl + 1) * 128], False, c == 3, ["kA", "qA"], [pss_r])
                                    pt = Bt[4 + pb2 + hf][:]
                                    pr_ = ("B", 4 + pb2 + hf)
                                    act(pt, pss[:], AF.Exp, [pss_r], [pr_], scale=0.125)
                                    tt("dve", pt.rearrange("p (h n) -> p h n", h=4), pt.rearrange("p (h n) -> p h n", h=4),
                                       mA[:, bl, sl, :].unsqueeze(1).to_broadcast([128, 4, 128]), ALU.mult, [pr_, "mA"], [pr_])

                            def s2(b=b, sl=sl, kt=kt, pb2=pb2, pos=pos):
                                for hf in range(2):
                                    po, po_r = pos[hf]
                                    for c in range(4):
                                        mm(po[:, c * 128:(c + 1) * 128], vA[:, kt, 2 * c + hf, :], Bt[4 + pb2 + hf][:, c * 128:(c + 1) * 128],
                                           sl == 0 and c == 0, sl == 6 and c == 3, ["vA", "vA1", ("B", 4 + pb2 + hf)], [po_r])
                                if sl == 6:
                                    for hf in range(2):
                                        po, po_r = pos[hf]
                                        dn, dn_r = F(nxt("dn", 2))
                                        yi = nxt("yst", 4)
                                        yst = Bt[yi][0:64, :]
                                        recip(dn[0:64, :], po[64:128, :], [po_r], [dn_r])
                                        tt("dve", yst, po[0:64, :], dn[0:64, :], ALU.mult, [po_r, dn_r], [("B", yi)])
                                        dma("pool", YT[YA0:YA0 + 4, hf * 64:(hf + 1) * 64, b * 128:(b + 1) * 128].rearrange("c p n -> p c n"),
                                            yst.rearrange("p (c n) -> p c n", c=4), [("B", yi)], [("YT", YA0, b)], "yst%d" % yi)
                            pipe.unit(s1, s2)
                    pipe.flush()
                S.barrier()

            if "B" in ph:
                acc = avf(0, 8192).rearrange("p (s n) -> p s n", s=4)
                kB = av(16384, 12288).rearrange("p (c n) -> p c n", c=2)
                qB = av(28672, 4096).rearrange("p (c n) -> p c n", c=2)
                vB = av(32768, 9216).rearrange("p (t s n) -> p t s n", t=18, s=4)
                memset("dve", vB[:, :, :, 64:128], 1.0, ["vB1"])
                for w in range(NW):
                    w0 = w * 2048
                    for g, dil in enumerate((1, 4, 16)):
                        halo = 128 * dil
                        lo = max(w0 - halo, 0)
                        hi = min(w0 + 2048 + halo, T)
                        off = lo - (w0 - halo)
                        dma("sp", qB, QK[QB0 + 2 * g:QB0 + 2 * g + 2, :, w0:w0 + 2048].rearrange("c p n -> p c n"),
                            [("QK", QB0 + 2 * g + c, 2 * w + i) for c in range(2) for i in range(2)], ["qB"], "aq")
                        if off > 0:
                            memset("dve", kB[:, :, 0:off], 0.0, ["kB"])
                        if off + hi - lo < 2048 + 2 * halo:
                            memset("dve", kB[:, :, off + hi - lo:2048 + 2 * halo], 0.0, ["kB"])
                        dma("sp", kB[:, :, off:off + hi - lo], QK[KB0 + 2 * g:KB0 + 2 * g + 2, :, lo:hi].rearrange("c p n -> p c n"),
                            [("QK", KB0 + 2 * g + c, b) for c in range(2) for b in range(T // 1024)], ["kB"], "ak")
                        nrun = 16 // dil
                        pipeB = Pipe()
                        for r in range(dil):
                            mi0 = w0 // (128 * dil)
                            nmi = NT // dil
                            vb0 = (nxt("vBh", 2) * 9) if nrun + 2 <= 9 else 0
                            for tq in range(nrun + 2):
                                mi = mi0 - 1 + tq
                                if mi < 0 or mi >= nmi:
                                    S.add("dve", lambda e, a=vB[:, vb0 + tq, :, 0:64]: e.memset(a, 0.0), (), [("vB", vb0)], tag=("vB", w, g, r))
                                    continue
                                tokb = dil * 128 * mi + r
                                src = bass.AP(tensor=VTM_h, offset=tokb * 1536 + 512 + 256 * g,
                                              ap=[[1536 * dil, 128], [64, 4], [1, 64]])
                                dma("sp", vB[:, vb0 + tq, :, 0:64], src, [("VTM", b) for b in range(T // 1024)], [("vB", vb0)],
                                    "avv%d" % (vb0 // 9), tag=("vB", w, g, r))
                            for ml in range(nrun):
                                mi = mi0 + ml
                                qoff = r + dil * 128 * ml
                                pobk = 4 + nxt("poB", 4)
                                for ri in range(3):
                                    koff = halo + r + dil * 128 * (ml + ri - 1)
                                    sb2 = 2 * nxt("pss", 2)
                                    pi = nxt("pb", 4)
                                    mcol = 3 * NT + g * 3 * NT + (r * nmi + mi) * 3 + ri
                                    vt = vb0 + ml + ri

                                    def s1(g=g, dil=dil, ri=ri, koff=koff, qoff=qoff, sb2=sb2, pi=pi, mcol=mcol):
                                        for hf in range(2):
                                            pss, pss_r = ps(sb2 + hf)
                                            bo = ((g * 3 + ri) * 2 + hf) * 256
                                            mm(pss[:, 0:256], identb[:], biasb[:, bo:bo + 256], True, False, ["identb", "biasb"], [pss_r])
                                            for j in range(2):
                                                mm(pss[:, j * 128:(j + 1) * 128],
                                                   kB[hf * 64:(hf + 1) * 64, j, koff:koff + 127 * dil + 1:dil],
                                                   qB[hf * 64:(hf + 1) * 64, j, qoff:qoff + 127 * dil + 1:dil], False, j == 1, ["kB", "qB"], [pss_r])
                                            act(Bt[4 + pi][:, hf * 256:(hf + 1) * 256], pss[:, 0:256], AF.Exp, [pss_r, "mcols"], [("B", 4 + pi)],
                                                bias=mcols[:, mcol:mcol + 1], scale=0.125)

                                    def s2(g=g, dil=dil, ri=ri, qoff=qoff, pi=pi, vt=vt, vb0=vb0, pobk=pobk):
                                        po, po_r = ps(pobk)
                                        for s_ in range(4):
                                            pj = (s_ % 2) * 2 + s_ // 2
                                            mm(po[:, s_ * 128:(s_ + 1) * 128], vB[:, vt, s_, :], Bt[4 + pi][:, pj * 128:(pj + 1) * 128],
                                               ri == 0 and s_ == 0, ri == 2 and s_ == 3, [("vB", vb0), "vB1", ("B", 4 + pi)], [po_r])
                                        if ri == 2:
                                            accv = acc[:, :, qoff:qoff + 127 * dil + 1:dil]
                                            pov = po[:].rearrange("p (s n) -> p s n", s=4)
                                            if g == 0:
                                                copy("dve", accv, pov, [po_r], ["acc"])
                                            else:
                                                tt("dve", accv, pov, accv, ALU.add, [po_r, "acc"], ["acc"])
                                    pipeB.unit(s1, s2)
                        pipeB.flush()
                    for s_ in range(4):
                        for hfw in range(4):
                            dn, dn_r = F(nxt("dn", 2))
                            yi = nxt("yst", 4)
                            yst = Bt[yi][0:64, :]
                            recip(dn[0:64, :], acc[64:128, s_, hfw * 512:(hfw + 1) * 512], ["acc"], [dn_r])
                            tt("dve", yst, acc[0:64, s_, hfw * 512:(hfw + 1) * 512], dn[0:64, :], ALU.mult, ["acc", dn_r], [("B", yi)])
                            dma("pool", YT[YB0 + s_ // 2, (s_ % 2) * 64:(s_ % 2 + 1) * 64, w0 + hfw * 512:w0 + (hfw + 1) * 512], yst, [("B", yi)],
                                [("YT", YB0 + s_ // 2, w)], "yst%d" % yi)
                S.barrier()

            if "p3" in ph:
                hT3 = av(0, 8192).rearrange("p (k n) -> p k n", k=16)
                yT3 = av(8192, 7168).rearrange("p (k n) -> p k n", k=14)
                mg = av(15360, 8192).rearrange("p (k n) -> p k n", k=16)
                bsls = [av(23552 + i * 7168, 7168).rearrange("p (k n) -> p k n", k=14) for i in range(2)]
                BRK = [(0, 4), (4, 2), (6, 4), (10, 4)]
                for blk in range(T // 512):
                    t0 = blk * 512
                    dma("sp", hT3, HT[:, :, t0:t0 + 512].rearrange("k p n -> p k n"), [("HT", blk // 2)], ["hT3"], "a3h")
                    dma("sp", yT3, YT[:, :, t0:t0 + 512].rearrange("k p n -> p k n"), [], ["yT3"], "a3y")
                    for jq in range(4):
                        bi = nxt("bsl", 2)
                        bsl = bsls[bi]
                        dma("sp", bsl, WB[l][S_BR + jq][:, 0:7168].rearrange("p (k n) -> p k n", k=14), [("wb", l, S_BR + jq)], [("bsl", bi)],
                            "a3b" if bi == 0 else "a3c")
                        for jj in range(4):
                            j = 4 * jq + jj
                            k = load_slab(l, S_G + j)
                            gv = wbuf[k][:].rearrange("p (k n) -> p k n", k=16)
                            accf, acc_r = F(12 + nxt("macc", 2))
                            for i in range(4):
                                pg, pg_r = ps(nxt("pg", 2))
                                for kc in range(16):
                                    mm(pg[:], gv[:, kc, i * 128:(i + 1) * 128], hT3[:, kc, :], kc == 0, kc == 15, [("wbuf", k), "hT3"], [pg_r])
                                pp, pp_r = ps(2 + nxt("pp", 2))
                                k0, nk = BRK[i]
                                for u in range(nk):
                                    mm(pp[:], bsl[:, k0 + u, jj * 128:(jj + 1) * 128], yT3[:, k0 + u, :], u == 0, u == nk - 1, [("bsl", bi), "yT3"], [pp_r])
                                sg, sg_r = F(8 + nxt("sg", 2))
                                act(sg, pg[:], AF.Sigmoid, [pg_r], [sg_r])
                                if i == 0:
                                    tt("dve", accf, pp[:], sg, ALU.mult, [pp_r, sg_r], [acc_r])
                                else:
                                    tmp, tmp_r = F(10 + nxt("mtmp", 2))
                                    tt("dve", tmp, pp[:], sg, ALU.mult, [pp_r, sg_r], [tmp_r])
                                    if i < 3:
                                        tt("dve", accf, accf, tmp, ALU.add, [acc_r, tmp_r], [acc_r])
                                    else:
                                        tt("dve", mg[:, j, :], accf, tmp, ALU.add, [acc_r, tmp_r], ["mg"])
                    for s in range(4):
                        k = load_slab(l, S_WO + s)
                        wv = wbuf[k][:].rearrange("p (k n) -> p k n", k=16)
                        for jj in range(4):
                            j = 4 * s + jj
                            po2, po2_r = ps(4 + nxt("po3", 2))
                            for kc in range(16):
                                mm(po2[:], wv[:, kc, jj * 128:(jj + 1) * 128], mg[:, kc, :], kc == 0, kc == 15, [("wbuf", k), "mg"], [po2_r])
                            xc, xc_r = F(nxt("xc", 2))
                            dma("sp", xc, XT[j, :, t0:t0 + 512], [("XT", blk, j)], [xc_r], "xc%d" % (rot["xc"] % 2))
                            xn, xn_r = F(2 + nxt("xn", 2))
                            tt("dve", xn, po2[:], xc, ALU.add, [po2_r, xc_r], [xn_r])
                            dma("pool", XT[j, :, t0:t0 + 512], xn, [xn_r], [("XT", blk, j)], "xn%d" % (rot["xn"] % 2))
                S.barrier()

            if "mlp" in ph:
                h2 = av(0, 8192).rearrange("p (k n) -> p k n", k=16)
                hid = av(8192, 32768).rearrange("p (k n) -> p k n", k=64)
                for blk in range(T // 512):
                    t0 = blk * 512
                    norm_fm(t0, gmlp, "gmlp", h2, "h2")
                    for s in range(16):
                        k = load_slab(l, S_UP + s)
                        wv = wbuf[k][:].rearrange("p (k n) -> p k n", k=16)
                        for ci in range(4):
                            pu, pu_r = ps(nxt("pu", 4))
                            for kc in range(16):
                                mm(pu[:], wv[:, kc, ci * 128:(ci + 1) * 128], h2[:, kc, :], kc == 0, kc == 15, [("wbuf", k), "h2"], [pu_r])
                            rl, rl_r = F(8 + nxt("rl", 4))
                            act(rl, pu[:], AF.Relu, [pu_r], [rl_r])
                            tt("dve" if ci % 2 else "pool", hid[:, 4 * s + ci, :], rl, rl, ALU.mult, [rl_r], [("hid", 4 * s + ci)])
                    for j in range(16):
                        k = load_slab(l, S_DN + j)
                        wv = wbuf[k][:].rearrange("p (k n) -> p k n", k=64)
                        pd, pd_r = ps(4 + nxt("pd", 2))
                        for kc in range(64):
                            mm(pd[:], wv[:, kc, :], hid[:, kc, :], kc == 0, kc == 63, [("wbuf", k), ("hid", kc)], [pd_r])
                        xc, xc_r = F(12 + nxt("xc2", 2))
                        dma("sp", xc, XT[j, :, t0:t0 + 512], [("XT", blk, j)], [xc_r], "xc%d" % (rot["xc2"] % 2))
                        xn, xn_r = F(14 + nxt("xn2", 2))
                        tt("dve", xn, pd[:], xc, ALU.add, [pd_r, xc_r], [xn_r])
                        dma("pool", XT[j, :, t0:t0 + 512], xn, [xn_r], [("XT", blk, j)], "xn%d" % (rot["xn2"] % 2))
                S.barrier()

        if "fin" in ph:
            gfin = avf(0, 2048)
            dma("sp", gfin, norm_final.rearrange("(o n) -> o n", o=1).to_broadcast([128, D]), [], ["gfin"], "c0", slow=True)
            for t in range(NT):
                b2 = t % 2
                xt = avf(4096 + b2 * 4096, 2048).rearrange("p (k n) -> p k n", k=16)
                xtm = avf(12288 + b2 * 4096, 2048)
                yo = avf(20480 + b2 * 4096, 2048)
                dma("sp", xt, XT[:, :, t * 128:(t + 1) * 128].rearrange("k p n -> p k n"), [], [("fxt", b2)], "xin%d" % b2)
                for q in range(4):
                    pi = b2 * 4 + q
                    for j in range(4):
                        kc = 4 * q + j
                        S.add("pe", lambda e, o=PS[pi][:, j * 128:(j + 1) * 128], i=xt[:, kc, :]: e.transpose(o, i, identf),
                              [("fxt", b2), "cf32"], [("ps", pi)])
                    copy(evac_eng(), xtm[:, q * 512:(q + 1) * 512], PS[pi][:], [("ps", pi)], [("xtm", b2, q)])
                jk, jk_r = F(nxt("fj", 2))
                ss = small[:, b2:b2 + 1]
                for q in range(4):
                    act(jk, xtm[:, q * 512:(q + 1) * 512], AF.Square, [("xtm", b2, q)], [jk_r], accum_out=small[:, 4 + q:5 + q])
                tt("dve", small[:, 4:5], small[:, 4:5], small[:, 5:6], ALU.add, [jk_r], [jk_r])
                tt("dve", small[:, 6:7], small[:, 6:7], small[:, 7:8], ALU.add, [jk_r], [jk_r])
                tt("dve", ss, small[:, 4:5], small[:, 6:7], ALU.add, [jk_r], [("ss", b2)])
                act(ss, ss, AF.Sqrt, [("ss", b2), "epsb"], [("ss", b2)], bias=epsb[:], scale=1.0 / D)
                recip(ss, ss, [("ss", b2)], [("ss", b2)])
                stt(yo, xtm, ss, gfin, ALU.mult, ALU.mult, [("xtm", b2, q) for q in range(4)] + [("ss", b2), "gfin"], [("yo", b2)])
                dma("pool", y_out[t * 128:(t + 1) * 128, :], yo, [("yo", b2)], ["yout"], "xst%d" % b2)
        S.barrier()
        S.add("sp", lambda e: e.nop(), (), ())
        S.emit(nc, st)
    return nc, S


T_FULL = 8192
DEPTH = 4
_static = None


def kernel(x_prompt, x_sample, norm_mix, w_in, rel_bias_a, q_gain_c, k_gain_c, sink_d, w_branch_a, w_branch_b,
           w_branch_c, w_branch_d, w_out, norm_mlp, w_up, w_down, norm_final):
    f = lambda a: np.ascontiguousarray(np.asarray(a, dtype=np.float32))
    x_prompt = f(x_prompt)
    x_sample = f(x_sample)
    T = T_FULL
    stt = static_tables()
    tab_p = core_tables([8192], T)
    tab_s = core_tables([2048] * 4, T)
    shared = {"norm_mix": f(norm_mix), "w_in": f(w_in), "rel_bias_a": f(rel_bias_a), "q_gain_c": f(q_gain_c),
              "k_gain_c": f(k_gain_c), "sink_d": f(sink_d), "w_branch_a": f(w_branch_a), "w_branch_b": f(w_branch_b),
              "w_branch_c": f(w_branch_c), "w_branch_d": f(w_branch_d), "w_out": f(w_out), "norm_mlp": f(norm_mlp),
              "w_up": f(w_up), "w_down": f(w_down), "norm_final": f(norm_final)}
    shared.update(stt)
    work = {0: (x_prompt[0], tab_p), 1: (x_prompt[1], tab_p), 4: (x_sample.reshape(T, D), tab_s)}
    zeros = np.zeros((T, D), np.float32)
    in_maps = []
    for c in range(8):
        xc, tb = work.get(c, (zeros, tab_p))
        m = dict(shared)
        m["x"] = xc
        m.update(tb)
        in_maps.append(m)
    nc, _ = build_program(T, DEPTH)
    res = run_bass_kernel_spmd(nc, in_maps, core_ids=list(range(8)))
    y0 = np.asarray(res.results[0]["y"], dtype=np.float32)
    y1 = np.asarray(res.results[1]["y"], dtype=np.float32)
    y2 = np.asarray(res.results[4]["y"], dtype=np.float32)
    y_prompt = np.stack([y0, y1], axis=0)
    y_sample = y2.reshape(4, 2048, D)
    return (y_prompt, y_sample)
```
